# Optimizing a Trainium2 kernel written in Bass

```python
import math
import jax, jax.numpy as jnp
from jax import lax
import numpy as np

D_MODEL = 1024
BATCH = 8
SEQ = 8192
DEPTH = 1

CHUNK = 64
Q_BLOCK = 128
CONV_WIDTH = D_MODEL // 2
ATTN_WIDTH = D_MODEL - CONV_WIDTH
N_DIFF_HEADS = 4
DIFF_HEAD_DIM = ATTN_WIDTH // (2 * N_DIFF_HEADS)
CONV_KERNEL = 31
FFN_DIM = 2816
FFN_CONV_KERNEL = 3
ROPE_THETA = 10000.0
RMS_EPS = 1e-6
LN_EPS = 1e-5
SUBLN_EPS = 1e-5
IN_COLS = 2 * CONV_WIDTH + 3 * ATTN_WIDTH

kernel_name = "hymba_conformer_diffattn_convglu_block"


def rms_norm(x, g, eps=RMS_EPS):
    xf = x.astype(jnp.float32)
    y = xf * lax.rsqrt(jnp.mean(xf * xf, axis=-1, keepdims=True) + eps)
    return (y * g.astype(jnp.float32)).astype(x.dtype)


def layer_norm(x, g, b, eps=LN_EPS):
    xf = x.astype(jnp.float32)
    mu = jnp.mean(xf, axis=-1, keepdims=True)
    var = jnp.mean(jnp.square(xf - mu), axis=-1, keepdims=True)
    y = (xf - mu) * lax.rsqrt(var + eps)
    return (y * g.astype(jnp.float32) + b.astype(jnp.float32)).astype(x.dtype)


def causal_dwconv(u, w, b):
    k = w.shape[0]
    c = u.shape[-1]
    out = lax.conv_general_dilated(
        u, w[:, None, :].astype(u.dtype), window_strides=(1,), padding=[(k - 1, 0)],
        dimension_numbers=("NWC", "WIO", "NWC"), feature_group_count=c)
    return out + b.astype(u.dtype)


def rope_tables(seq, dim):
    inv_freq = 1.0 / (ROPE_THETA ** (jnp.arange(0, dim, 2, dtype=jnp.float32) / dim))
    ang = jnp.arange(seq, dtype=jnp.float32)[:, None] * inv_freq[None, :]
    return jnp.cos(ang), jnp.sin(ang)


def apply_rope(t, cos, sin):
    tf = t.astype(jnp.float32)
    half = tf.shape[-1] // 2
    t1, t2 = tf[..., :half], tf[..., half:]
    c = cos[None, :, None, None, :]
    s = sin[None, :, None, None, :]
    out = jnp.concatenate([t1 * c - t2 * s, t2 * c + t1 * s], axis=-1)
    return out.astype(t.dtype)


def diff_attention(q, k, v, lam):
    b, s, h, _, dh = q.shape
    nblk = s // Q_BLOCK
    scale = dh ** -0.5
    q_blocks = q.reshape(b, nblk, Q_BLOCK, h, 2, dh).swapaxes(0, 1)
    key_chunk = jnp.arange(s) // CHUNK

    def one_block(args):
        qb, blk = args
        q_chunk = (blk * Q_BLOCK + jnp.arange(Q_BLOCK)) // CHUNK
        mask = key_chunk[None, :] <= q_chunk[:, None]
        scores = jnp.einsum("bqhmd,bkhmd->bhmqk", qb, k).astype(jnp.float32) * scale
        scores = jnp.where(mask, scores, -jnp.inf)
        p = jax.nn.softmax(scores, axis=-1)
        w = p[:, :, 0] - lam * p[:, :, 1]
        return jnp.einsum("bhqk,bkhe->bqhe", w.astype(v.dtype), v)

    out = lax.map(one_block, (q_blocks, jnp.arange(nblk)))
    return out.swapaxes(0, 1).reshape(b, s, h, 2 * dh)


def setup_inputs(seed: int = 0) -> dict:
    key = jax.random.key(seed)
    ks = jax.random.split(key, 24)
    f32 = jnp.float32

    def nrm(k, shape, scale):
        return jax.random.normal(k, shape, f32) * scale

    dh = DIFF_HEAD_DIM
    return {
        "x": nrm(ks[0], (BATCH, SEQ, D_MODEL), 1.0),
        "g_mix": 1.0 + nrm(ks[1], (DEPTH, D_MODEL), 0.02),
        "w_in": nrm(ks[2], (DEPTH, D_MODEL, IN_COLS), D_MODEL ** -0.5),
        "w_dw": nrm(ks[3], (DEPTH, CONV_KERNEL, CONV_WIDTH), CONV_KERNEL ** -0.5),
        "b_dw": nrm(ks[4], (DEPTH, CONV_WIDTH), 0.02),
        "ln_g": 1.0 + nrm(ks[5], (DEPTH, CONV_WIDTH), 0.02),
        "ln_b": nrm(ks[6], (DEPTH, CONV_WIDTH), 0.02),
        "lambda_q1": nrm(ks[7], (DEPTH, dh), 0.1),
        "lambda_k1": nrm(ks[8], (DEPTH, dh), 0.1),
        "lambda_q2": nrm(ks[9], (DEPTH, dh), 0.1),
        "lambda_k2": nrm(ks[10], (DEPTH, dh), 0.1),
        "g_subln": 1.0 + nrm(ks[11], (DEPTH, 2 * dh), 0.02),
        "w_out": nrm(ks[12], (DEPTH, CONV_WIDTH + ATTN_WIDTH, D_MODEL), (CONV_WIDTH + ATTN_WIDTH) ** -0.5),
        "g_ffn": 1.0 + nrm(ks[13], (DEPTH, D_MODEL), 0.02),
        "w_up": nrm(ks[14], (DEPTH, D_MODEL, 2 * FFN_DIM), D_MODEL ** -0.5),
        "w_fconv": nrm(ks[15], (DEPTH, FFN_CONV_KERNEL, FFN_DIM), FFN_CONV_KERNEL ** -0.5),
        "b_fconv": nrm(ks[16], (DEPTH, FFN_DIM), 0.02),
        "w_down": nrm(ks[17], (DEPTH, FFN_DIM, D_MODEL), FFN_DIM ** -0.5),
        "g_final": 1.0 + nrm(ks[18], (D_MODEL,), 0.02),
    }


def reference(x, g_mix, w_in, w_dw, b_dw, ln_g, ln_b, lambda_q1, lambda_k1, lambda_q2,
              lambda_k2, g_subln, w_out, g_ffn, w_up, w_fconv, b_fconv, w_down, g_final):
    b, s, _ = x.shape
    cos, sin = rope_tables(s, DIFF_HEAD_DIM)
    splits = [CONV_WIDTH, 2 * CONV_WIDTH, 2 * CONV_WIDTH + ATTN_WIDTH, 2 * CONV_WIDTH + 2 * ATTN_WIDTH]
    for layer in range(DEPTH):
        h = rms_norm(x, g_mix[layer])
        proj = h @ w_in[layer]
        c_val, c_gate, q, k, v = jnp.split(proj, splits, axis=-1)

        a = c_val * jax.nn.sigmoid(c_gate)
        a = causal_dwconv(a, w_dw[layer], b_dw[layer])
        a = jax.nn.silu(layer_norm(a, ln_g[layer], ln_b[layer]))

        q = apply_rope(q.reshape(b, s, N_DIFF_HEADS, 2, DIFF_HEAD_DIM), cos, sin)
        k = apply_rope(k.reshape(b, s, N_DIFF_HEADS, 2, DIFF_HEAD_DIM), cos, sin)
        v = v.reshape(b, s, N_DIFF_HEADS, 2 * DIFF_HEAD_DIM)
        lam_init = 0.8 - 0.6 * math.exp(-0.3 * layer)
        lam = (jnp.exp(jnp.sum(lambda_q1[layer].astype(jnp.float32) * lambda_k1[layer].astype(jnp.float32)))
               - jnp.exp(jnp.sum(lambda_q2[layer].astype(jnp.float32) * lambda_k2[layer].astype(jnp.float32)))
               + lam_init)
        o = diff_attention(q, k, v, lam)
        o = rms_norm(o, g_subln[layer], SUBLN_EPS) * (1.0 - lam_init)
        o = o.reshape(b, s, ATTN_WIDTH)

        x = x + jnp.concatenate([a, o], axis=-1) @ w_out[layer]

        h = rms_norm(x, g_ffn[layer])
        gate, val = jnp.split(h @ w_up[layer], 2, axis=-1)
        gate = causal_dwconv(gate, w_fconv[layer], b_fconv[layer])
        x = x + (jax.nn.silu(gate) * val) @ w_down[layer]
    return rms_norm(x, g_final)
```

```python
import math
from contextlib import ExitStack

import numpy as np
import concourse.bass as bass
import concourse.mybir as mybir
from concourse.bass_utils import run_bass_kernel_spmd

F32 = mybir.dt.float32
BF16 = mybir.dt.bfloat16
AF = mybir.ActivationFunctionType
ALU = mybir.AluOpType
AX = mybir.AxisListType

D = 1024
SEQ = 8192
NCORES = 8
FF = 2816
NF = FF // 128
INP = 3584
RMS_EPS = 1e-6
LN_EPS = 1e-5
SUBLN_EPS = 1e-5
LAM_INIT = 0.8 - 0.6 * math.exp(-0.3 * 0)


class Buf:
    __slots__ = ("name", "writers", "reads", "war")

    def __init__(self, name=""):
        self.name = name
        self.writers = []
        self.reads = []
        self.war = []


class Op:
    __slots__ = ("eng", "fn", "deps", "sig", "dkey", "dval", "needed")


class Prog:
    ENGS = ("sp", "act", "dve", "pool", "pe")

    def __init__(self):
        self.streams = {e: [] for e in self.ENGS}
        self.dma_counts = {}
        self.last_dma = {}
        self.bar = []
        self.keys = []

    def op(self, eng, fn, r=(), w=(), partial=False, dkey=None, deps=()):
        o = Op()
        o.eng = eng
        o.fn = fn
        o.sig = None
        o.needed = False
        if dkey == "auto":
            dkey = ("kl_" + w[0].name) if w else ("ks_" + r[0].name)
            if dkey not in self.dma_counts:
                self.keys.append(dkey)
        o.dkey = dkey
        o.dval = None
        d = list(self.bar)
        d.extend(x for x in deps if x is not None)
        for b in r:
            d.extend(b.writers)
        for b in w:
            if b.reads:
                b.war = b.reads
                b.reads = []
                b.writers = []
            elif not partial:
                d.extend(x for x in b.writers if x.eng != eng or x.dkey is not None)
                b.writers = []
            d.extend(x for x in b.war if x.eng != eng or x.dkey is not None)
            b.writers.append(o)
        for b in r:
            b.reads.append(o)
        o.deps = d
        if dkey is not None:
            self.dma_counts[dkey] = self.dma_counts.get(dkey, 0) + 1
            o.dval = 16 * self.dma_counts[dkey]
            self.last_dma[dkey] = o
        self.streams[eng].append(o)
        return o

    def barrier(self):
        b = []
        for e in self.ENGS:
            if e == "sp":
                continue
            for o in reversed(self.streams[e]):
                if o.dkey is None:
                    b.append(o)
                    break
        b.extend(self.last_dma.values())
        self.bar = b

    def emitter(self, engsem, dmasem):
        for e in self.ENGS:
            for o in self.streams[e]:
                for d in o.deps:
                    d.needed = True
        for e in self.ENGS:
            c = 0
            for o in self.streams[e]:
                if o.dkey is None and o.needed:
                    c += 1
                    o.sig = c

        def run(ename, e):
            waited = {}
            for o in self.streams[ename]:
                need = {}
                for d in o.deps:
                    if d.dkey is not None:
                        k = ("d", d.dkey)
                        v = d.dval
                    else:
                        k = ("e", d.eng)
                        v = d.sig
                    if need.get(k, 0) < v:
                        need[k] = v
                for k, v in need.items():
                    if waited.get(k, 0) < v:
                        sem = dmasem[k[1]] if k[0] == "d" else engsem[k[1]]
                        e.wait_ge(sem, v)
                        waited[k] = v
                ins = o.fn(e)
                if o.dkey is not None:
                    ins.then_inc(dmasem[o.dkey], 16)
                elif o.sig is not None:
                    ins.then_inc(engsem[ename], 1)

        return run


def build_nc(S=SEQ, dbg=False):
    nc = bass.Bass("TRN2", target_bir_lowering=False)
    NT = S // 512
    NCH = S // 128

    def din(name, shape):
        return nc.dram_tensor(name, shape, F32, kind="ExternalInput").ap()

    x_d = din("x", [S, D])
    w_in_d = din("w_in_p", [D, INP])
    w_out_d = din("w_out", [D, D])
    w_up_d = din("w_up", [D, 2 * FF])
    w_down_d = din("w_down", [FF, D])
    cos_d = din("cosT", [128, S])
    sin_d = din("sinT", [128, S])
    ident_d = din("ident", [128, 128])
    gmix_d = din("g_mix", [1, D])
    gffn_d = din("g_ffn", [1, D])
    gfin_d = din("g_final", [1, D])
    wdw_d = din("wdw_t", [512, 31])
    cvec_d = din("cvec", [128, 12])
    lamv_d = din("lamv", [1, 256])
    gsub_d = din("g_subln", [128, 1])
    wfc_d = din("wfc", [128, NF * 3])
    bfc_d = din("bfc", [128, NF])
    out_d = nc.dram_tensor("out", [S, D], F32, kind="ExternalOutput").ap()

    okind = "ExternalOutput" if dbg else "Internal"
    aT_d = nc.dram_tensor("aT_s", [4, 128, S], BF16, kind=okind).ap()
    qT_d = nc.dram_tensor("qT_s", [4, 128, S], BF16, kind=okind).ap()
    kT_d = nc.dram_tensor("kT_s", [4, 128, S], BF16, kind=okind).ap()
    v_d = nc.dram_tensor("v_s", [NCH, 128, 512], BF16, kind=okind).ap()
    oT_d = nc.dram_tensor("oT_s", [4, 128, S], BF16, kind=okind).ap()

    P = Prog()
    sem_keys = []

    def key(name):
        sem_keys.append(name)
        return name

    def lam(f, *a):
        return lambda e: f(e, *a)

    with ExitStack() as top:
        def mk(es, name, shape, dt):
            return es.enter_context(nc.sbuf_tensor("sb_" + name, shape, dt)), Buf(name)

        def mkp(es, name, shape, dt):
            return es.enter_context(nc.psum_tensor("ps_" + name, shape, dt)), Buf(name)

        ident, b_ident = mk(top, "ident", [128, 128], BF16)
        onesb, b_onesb = mk(top, "onesb", [128, 128], BF16)
        epsr, b_epsr = mk(top, "epsr", [128, 1], F32)
        epsl, b_epsl = mk(top, "epsl", [128, 1], F32)
        cvec, b_cvec = mk(top, "cvec", [128, 12], F32)
        gsub, b_gsub = mk(top, "gsub", [128, 1], F32)
        lams, b_lams = mk(top, "lams", [128, 2], F32)
        nlam, b_nlam = mk(top, "nlam", [128, 1], F32)

        with ExitStack() as pa:
            w_in, b_w_in = mk(pa, "w_in", [128, 8, INP], BF16)
            gmix, b_gmix = mk(pa, "gmix", [128, D], F32)
            wdw, b_wdw = mk(pa, "wdw", [128, 4, 31], F32)
            xt = [mk(pa, f"xt{i}", [128, 4, D], F32) for i in range(2)]
            junk, b_junk = mk(pa, "junk", [128, D], BF16)
            ss, b_ss = mk(pa, "ss", [128, 4], F32)
            rstd, b_rstd = mk(pa, "rstd", [128, 4], F32)
            hb = [mk(pa, f"hb{i}", [128, D], BF16) for i in range(2)]
            hT, b_hT = mk(pa, "hT", [128, 8, 512], BF16)
            glu, b_glu = mk(pa, "glu", [128, 4, 542], BF16)
            diag, b_diag = mk(pa, "diag", [128, 4 * 31, 128], BF16)
            th = [mk(pa, f"th{i}", [128, 512], F32) for i in range(2)]
            yc, _b = mk(pa, "yc", [128, 4, 512], F32)
            b_ycs = [Buf(f"yc{c}") for c in range(4)]
            ysq = [mk(pa, f"ysq{i}", [128, 512], F32) for i in range(2)]
            mean, b_mean = mk(pa, "mean", [128, 512], F32)
            m2, b_m2 = mk(pa, "m2", [128, 512], F32)
            lrs, b_lrs = mk(pa, "lrs", [128, 512], F32)
            zt = [mk(pa, f"zt{i}", [128, 512], F32) for i in range(2)]
            aout, b_aout = mk(pa, "aout", [128, 4, 512], BF16)
            cst = [mk(pa, f"cst{i}", [128, 2, 512], F32) for i in range(2)]
            t1 = [mk(pa, f"t1{i}", [128, 512], F32) for i in range(2)]
            t2 = [mk(pa, f"t2{i}", [128, 512], F32) for i in range(2)]
            qout, b_qout = mk(pa, "qout", [128, 4, 512], BF16)
            kout, b_kout = mk(pa, "kout", [128, 4, 512], BF16)
            vout, b_vout = mk(pa, "vout", [128, 4, 512], BF16)

            pT, b_pT = mkp(pa, "pT", [128, 8, 128], BF16)
            pA = [mkp(pa, f"pA{i}", [128, 512], F32) for i in range(2)]
            pB = [mkp(pa, f"pB{i}", [128, 512], F32) for i in range(2)]
            pS1, b_pS1 = mkp(pa, "pS1", [128, 512], F32)
            pS2, b_pS2 = mkp(pa, "pS2", [128, 512], F32)

            identf, b_identf = mk(pa, "identf", [128, 128], F32)
            onesf, b_onesf = mk(pa, "onesf", [128, 128], F32)
            lamv, b_lamv = mk(pa, "lamv", [128, 256], F32)
            lamp, b_lamp = mk(pa, "lamp", [128, 128], F32)
            k_c = key("const")
            P.op("sp", lambda e: e.dma_start(out=identf[:], in_=ident_d[:, :]), w=[b_identf], dkey="auto")
            P.op("sp", lambda e: e.dma_start(out=cvec[:], in_=cvec_d[:, :]), w=[b_cvec], dkey="auto")
            P.op("sp", lambda e: e.dma_start(out=gsub[:], in_=gsub_d[:, :]), w=[b_gsub], dkey="auto")
            P.op("sp", lambda e: e.dma_start(out=lamv[:], in_=lamv_d[0:1, :].partition_broadcast(128)), w=[b_lamv], dkey="auto")
            P.op("dve", lambda e: e.tensor_copy(out=ident[:], in_=identf[:]), r=[b_identf], w=[b_ident])
            P.op("pool", lambda e: e.memset(onesf[:], 1.0), w=[b_onesf])
            P.op("pool", lambda e: e.memset(onesb[:], 1.0), w=[b_onesb])
            P.op("pool", lambda e: e.memset(epsr[:], RMS_EPS), w=[b_epsr])
            P.op("pool", lambda e: e.memset(epsl[:], LN_EPS), w=[b_epsl])
            P.op("dve", lambda e: e.tensor_tensor(out=lamp[:, 0:64], in0=lamv[:, 0:64], in1=lamv[:, 64:128], op=ALU.mult), r=[b_lamv], w=[b_lamp], partial=True)
            P.op("dve", lambda e: e.tensor_tensor(out=lamp[:, 64:128], in0=lamv[:, 128:192], in1=lamv[:, 192:256], op=ALU.mult), r=[b_lamv], w=[b_lamp], partial=True)
            P.op("dve", lambda e: e.reduce_sum(out=lams[:, 0:2], in_=lamp[:].rearrange("p (a b) -> p a b", a=2), axis=AX.X), r=[b_lamp], w=[b_lams])
            P.op("act", lambda e: e.activation(out=lams[:], in_=lams[:], func=AF.Exp), r=[b_lams], w=[b_lams])
            P.op("dve", lambda e: e.tensor_tensor(out=nlam[:], in0=lams[:, 1:2], in1=lams[:, 0:1], op=ALU.subtract), r=[b_lams], w=[b_nlam])
            P.op("dve", lambda e: e.tensor_scalar(out=nlam[:], in0=nlam[:], scalar1=-LAM_INIT, scalar2=None, op0=ALU.add), r=[b_nlam], w=[b_nlam])
            P.op("dve", lambda e: e.tensor_scalar(out=gsub[:], in0=gsub[:], scalar1=1.0 - LAM_INIT, scalar2=None, op0=ALU.mult), r=[b_gsub], w=[b_gsub])

            k_w = key("wload")
            for kc in range(8):
                P.op("pool", lam(lambda e, kc: e.dma_start(out=w_in[:, kc, :], in_=w_in_d[kc * 128:(kc + 1) * 128, :]), kc),
                     w=[b_w_in], partial=True, dkey="auto")
            P.op("sp", lambda e: e.dma_start(out=gmix[:], in_=gmix_d[0:1, :].partition_broadcast(128)), w=[b_gmix], dkey="auto")
            P.op("sp", lambda e: e.dma_start(out=wdw[:], in_=wdw_d.rearrange("(c p) k -> p c k", p=128)), w=[b_wdw], dkey="auto")
            P.op("dve", lambda e: e.tensor_scalar(out=wdw[:], in0=wdw[:], scalar1=0.5, scalar2=None, op0=ALU.mult), r=[b_wdw], w=[b_wdw])
            P.op("pool", lambda e: e.memset(glu[:, :, 0:30], 0.0), w=[b_glu])
            for c in range(4):
                for k in range(31):
                    P.op("dve", lam(lambda e, c, k: e.tensor_scalar(out=diag[:, c * 31 + k, :], in0=identf[:], scalar1=wdw[:, c, k:k + 1], scalar2=None, op0=ALU.mult), c, k),
                         r=[b_identf, b_wdw], w=[b_diag], partial=True)

            k_x = [key("xa0"), key("xa1")]
            k_cs = [key("cs0"), key("cs1")]
            k_oa = key("outA")
            pair_ctr = [0]

            def load_x(t):
                xb, bx = xt[t % 2]
                P.op("sp", lam(lambda e, xb, t: e.dma_start(out=xb[:], in_=x_d[t * 512:(t + 1) * 512, :].rearrange("(s p) d -> p s d", p=128)), xb, t),
                     w=[bx], dkey="auto")

            def load_cs(t):
                cb, bc = cst[t % 2]
                P.op("sp", lam(lambda e, cb, t: e.dma_start(out=cb[:, 0, :], in_=cos_d[:, t * 512:(t + 1) * 512]), cb, t), w=[bc], partial=True, dkey="auto")
                P.op("sp", lam(lambda e, cb, t: e.dma_start(out=cb[:, 1, :], in_=sin_d[:, t * 512:(t + 1) * 512]), cb, t), w=[bc], partial=True, dkey="auto")

            load_x(0)
            load_cs(0)
            for t in range(NT):
                xb, bx = xt[t % 2]
                cb, bc = cst[t % 2]
                if t + 1 < NT:
                    load_x(t + 1)
                    load_cs(t + 1)
                for s in range(4):
                    P.op("act", lam(lambda e, xb, s: e.activation(out=junk[:], in_=xb[:, s, :], func=AF.Square, scale=1.0 / 32, accum_out=ss[:, s:s + 1]), xb, s),
                         r=[bx], w=[b_junk, b_ss], partial=True)
                P.op("act", lambda e: e.activation(out=ss[:], in_=ss[:], func=AF.Sqrt, bias=epsr[:, 0:1]), r=[b_ss, b_epsr], w=[b_ss])
                P.op("dve", lambda e: e.reciprocal(out=rstd[:], in_=ss[:]), r=[b_ss], w=[b_rstd])
                for s in range(4):
                    hbt, bh = hb[s % 2]
                    P.op("dve", lam(lambda e, hbt, xb, s: e.scalar_tensor_tensor(out=hbt[:], in0=xb[:, s, :], scalar=rstd[:, s:s + 1], in1=gmix[:], op0=ALU.mult, op1=ALU.mult), hbt, xb, s),
                         r=[bx, b_rstd, b_gmix], w=[bh])
                    for c in range(8):
                        P.op("pe", lam(lambda e, hbt, c: e.transpose(out=pT[:, c, :], in_=hbt[:, c * 128:(c + 1) * 128], identity=ident[:]), hbt, c),
                             r=[bh, b_ident], w=[b_pT], partial=True)
                    P.op("act", lam(lambda e, s: e.copy(out=hT[:, :, s * 128:(s + 1) * 128], in_=pT[:]), s), r=[b_pT], w=[b_hT], partial=True)

                def mm_fm(pt, bp, col0):
                    for kc in range(8):
                        P.op("pe", lam(lambda e, pt, kc, col0: e.matmul(pt[:], lhsT=w_in[:, kc, col0:col0 + 128], rhs=hT[:, kc, :], start=(kc == 0), stop=(kc == 7)), pt, kc, col0),
                             r=[b_w_in, b_hT], w=[bp], partial=True)

                for c in range(4):
                    i = pair_ctr[0] % 2
                    pair_ctr[0] += 1
                    (pa_, bpa), (pb_, bpb) = pA[i], pB[i]
                    tht, bth = th[c % 2]
                    mm_fm(pa_, bpa, c * 128)
                    mm_fm(pb_, bpb, 512 + c * 128)
                    P.op("act", lam(lambda e, tht, pb_: e.activation(out=tht[:], in_=pb_[:], func=AF.Tanh, scale=0.5), tht, pb_), r=[bpb], w=[bth])
                    P.op("dve", lam(lambda e, c, tht, pa_: e.scalar_tensor_tensor(out=glu[:, c, 30:542], in0=tht[:], scalar=1.0, in1=pa_[:], op0=ALU.add, op1=ALU.mult), c, tht, pa_),
                         r=[bth, bpa], w=[b_glu], partial=True)
                for which, col0, ot, bo in ((0, 1024, qout, b_qout), (1, 2048, kout, b_kout)):
                    for h in range(4):
                        i = pair_ctr[0] % 2
                        pair_ctr[0] += 1
                        (pa_, bpa), (pb_, bpb) = pA[i], pB[i]
                        t1t, bt1 = t1[h % 2]
                        t2t, bt2 = t2[h % 2]
                        mm_fm(pa_, bpa, col0 + h * 128)
                        mm_fm(pb_, bpb, col0 + 512 + h * 128)
                        P.op("dve", lam(lambda e, t1t, pa_, cb: e.tensor_tensor(out=t1t[:], in0=pa_[:], in1=cb[:, 0, :], op=ALU.mult), t1t, pa_, cb), r=[bpa, bc], w=[bt1])
                        P.op("dve", lam(lambda e, t2t, pb_, cb: e.tensor_tensor(out=t2t[:], in0=pb_[:], in1=cb[:, 1, :], op=ALU.mult), t2t, pb_, cb), r=[bpb, bc], w=[bt2])
                        P.op("pool", lam(lambda e, ot, h, t1t, t2t: e.tensor_tensor(out=ot[:, h, :], in0=t1t[:], in1=t2t[:], op=ALU.add), ot, h, t1t, t2t),
                             r=[bt1, bt2], w=[bo], partial=True)
                for s in range(4):
                    i = pair_ctr[0] % 2
                    pair_ctr[0] += 1
                    pa_, bpa = pA[i]
                    for kc in range(8):
                        P.op("pe", lam(lambda e, pa_, kc, s: e.matmul(pa_[:], lhsT=hT[:, kc, s * 128:(s + 1) * 128], rhs=w_in[:, kc, 3072:3584], start=(kc == 0), stop=(kc == 7)), pa_, kc, s),
                             r=[b_w_in, b_hT], w=[bpa], partial=True)
                    P.op("act", lam(lambda e, s, pa_: e.copy(out=vout[:, s, :], in_=pa_[:]), s, pa_), r=[bpa], w=[b_vout], partial=True)
                P.op("sp", lam(lambda e, t: e.dma_start(out=qT_d[:, :, t * 512:(t + 1) * 512].rearrange("h p s -> p h s"), in_=qout[:]), t), r=[b_qout], dkey="auto")
                P.op("sp", lam(lambda e, t: e.dma_start(out=kT_d[:, :, t * 512:(t + 1) * 512].rearrange("h p s -> p h s"), in_=kout[:]), t), r=[b_kout], dkey="auto")
                P.op("sp", lam(lambda e, t: e.dma_start(out=v_d[t * 4:(t + 1) * 4, :, :].rearrange("c p e -> p c e"), in_=vout[:]), t), r=[b_vout], dkey="auto")
                for c in range(4):
                    pc_, bpc = (pA[c // 2] if c % 2 == 0 else pB[c // 2])
                    for k in range(31):
                        P.op("pe", lam(lambda e, pc_, c, k: e.matmul(pc_[:], lhsT=diag[:, c * 31 + k, :], rhs=glu[:, c, k:k + 512], start=(k == 0), stop=(k == 30)), pc_, c, k),
                             r=[b_diag, b_glu], w=[bpc], partial=True)
                    P.op("dve", lam(lambda e, pc_, c: e.tensor_scalar(out=yc[:, c, :], in0=pc_[:], scalar1=cvec[:, c:c + 1], scalar2=None, op0=ALU.add), pc_, c),
                         r=[bpc, b_cvec], w=[b_ycs[c]])
                P.op("pool", lambda e: e.tensor_copy(out=glu[:, :, 0:30], in_=glu[:, :, 512:542]), r=[b_glu], w=[b_glu])
                for c in range(4):
                    yq, byq = ysq[c % 2]
                    P.op("act", lam(lambda e, yq, c: e.activation(out=yq[:], in_=yc[:, c, :], func=AF.Square), yq, c), r=[b_ycs[c]], w=[byq])
                    P.op("pe", lam(lambda e, c: e.matmul(pS1[:], lhsT=onesf[:], rhs=yc[:, c, :], start=(c == 0), stop=(c == 3)), c), r=[b_onesf, b_ycs[c]], w=[b_pS1], partial=True)
                    P.op("pe", lam(lambda e, c, yq: e.matmul(pS2[:], lhsT=onesf[:], rhs=yq[:], start=(c == 0), stop=(c == 3)), c, yq), r=[b_onesf, byq], w=[b_pS2], partial=True)
                P.op("dve", lambda e: e.tensor_scalar(out=mean[:], in0=pS1[:], scalar1=1.0 / 512, scalar2=None, op0=ALU.mult), r=[b_pS1], w=[b_mean])
                P.op("dve", lambda e: e.tensor_tensor(out=m2[:], in0=mean[:], in1=mean[:], op=ALU.mult), r=[b_mean], w=[b_m2])
                P.op("dve", lambda e: e.scalar_tensor_tensor(out=m2[:], in0=pS2[:], scalar=1.0 / 512, in1=m2[:], op0=ALU.mult, op1=ALU.subtract), r=[b_pS2, b_m2], w=[b_m2])
                P.op("act", lambda e: e.activation(out=m2[:], in_=m2[:], func=AF.Sqrt, bias=epsl[:, 0:1]), r=[b_m2, b_epsl], w=[b_m2])
                P.op("dve", lambda e: e.reciprocal(out=lrs[:], in_=m2[:]), r=[b_m2], w=[b_lrs])
                for c in range(4):
                    z, bz = zt[c % 2]
                    P.op("dve", lam(lambda e, z, c: e.tensor_tensor(out=z[:], in0=yc[:, c, :], in1=mean[:], op=ALU.subtract), z, c), r=[b_ycs[c], b_mean], w=[bz])
                    P.op("pool", lam(lambda e, z: e.tensor_tensor(out=z[:], in0=z[:], in1=lrs[:], op=ALU.mult), z), r=[bz, b_lrs], w=[bz])
                    P.op("act", lam(lambda e, z, c: e.activation(out=aout[:, c, :], in_=z[:], func=AF.Silu, scale=cvec[:, 4 + c:5 + c], bias=cvec[:, 8 + c:9 + c]), z, c),
                         r=[bz, b_cvec], w=[b_aout], partial=True)
                P.op("sp", lam(lambda e, t: e.dma_start(out=aT_d[:, :, t * 512:(t + 1) * 512].rearrange("c p s -> p c s"), in_=aout[:]), t), r=[b_aout], dkey="auto")
        P.barrier()

        with ExitStack() as pb:
            kT, b_kT = mk(pb, "kT", [128, 4, S], BF16)
            vv, b_vv = mk(pb, "vv", [128, NCH, 512], BF16)
            qt = [mk(pb, f"qt{i}", [128, 512], BF16) for i in range(2)]
            et = [mk(pb, f"et{i}", [128, 2, 512], BF16) for i in range(3)]
            eacc, _b = mk(pb, "eacc", [128, 2, 512], F32)
            b_eacc = [Buf("eacc0"), Buf("eacc1")]
            onesf2, b_onesf2 = mk(pb, "onesf2", [128, 128], F32)
            r1, b_r1 = mk(pb, "r1", [128, 512], F32)
            r2, b_r2 = mk(pb, "r2", [128, 512], F32)
            o1, b_o1 = mk(pb, "o1", [128, 512], F32)
            o2, b_o2 = mk(pb, "o2", [128, 512], F32)
            osq, b_osq = mk(pb, "osq", [128, 512], BF16)
            ors, b_ors = mk(pb, "ors", [128, 512], F32)
            ob = [mk(pb, f"ob{i}", [128, 512], BF16) for i in range(2)]
            epss, b_epss = mk(pb, "epss", [128, 1], F32)

            pSc = [mkp(pb, f"pSc{i}", [128, 2, 512], F32) for i in range(2)]
            pO1, b_pO1 = mkp(pb, "pO1", [128, 512], F32)
            pO2, b_pO2 = mkp(pb, "pO2", [128, 512], F32)
            pR1, b_pR1 = mkp(pb, "pR1", [128, 512], F32)
            pR2, b_pR2 = mkp(pb, "pR2", [128, 512], F32)

            P.op("pool", lambda e: e.memset(epss[:], SUBLN_EPS), w=[b_epss])
            P.op("pool", lambda e: e.memset(onesf2[:], 1.0), w=[b_onesf2])
            for h in range(4):
                P.op("sp", lam(lambda e, h: e.dma_start(out=kT[:, h, :], in_=kT_d[h, :, :]), h), w=[b_kT], partial=True, dkey="auto")
            VG = 8
            for c0 in range(0, NCH, VG):
                P.op("sp", lam(lambda e, c0: e.dma_start(out=vv[:, c0:c0 + VG, :], in_=v_d[c0:c0 + VG, :, :].rearrange("c p e -> p c e")), c0),
                     w=[b_vv], partial=True, dkey="auto")
            scale = 64 ** -0.5
            it = 0
            ectr = 0
            sctr = 0
            deferred = []

            def flush():
                for f in deferred:
                    f()
                deferred.clear()

            for h in range(4):
                for t in range(NT):
                    qb, bq = qt[it % 2]
                    P.op("sp", lam(lambda e, qb, h, t: e.dma_start(out=qb[:], in_=qT_d[h, :, t * 512:(t + 1) * 512]), qb, h, t), w=[bq], dkey="auto")
                    nk = 4 * (t + 1)
                    pend = None
                    for j in range(nk + 1):
                        if j < nk:
                            jd = j - 4 * t
                            c0 = 128 * jd if jd > 0 else 0
                            ps, bps = pSc[sctr % 2]
                            sctr += 1
                            eb, beb = et[ectr % 3]
                            ectr += 1
                            for m in range(2):
                                P.op("pe", lam(lambda e, ps, m, h, j, qb, c0: e.matmul(ps[:, m, c0:512], lhsT=kT[m * 64:(m + 1) * 64, h, j * 128:(j + 1) * 128], rhs=qb[m * 64:(m + 1) * 64, c0:512], start=True, stop=True), ps, m, h, j, qb, c0),
                                     r=[b_kT, bq], w=[bps], partial=True)
                            P.op("act", lam(lambda e, eb, ps, c0: e.activation(out=eb[:, :, c0:512], in_=ps[:, :, c0:512], func=AF.Exp, scale=scale), eb, ps, c0), r=[bps], w=[beb])
                            if jd >= 0:
                                P.op("pool", lam(lambda e, eb, c0: e.memset(eb[64:128, :, c0:c0 + 64], 0.0), eb, c0), r=[beb], w=[beb], partial=True)
                            for m, eng in ((0, "dve"), (1, "pool")):
                                if j == 0:
                                    P.op(eng, lam(lambda e, eb, m: e.tensor_copy(out=eacc[:, m, :], in_=eb[:, m, :]), eb, m), r=[beb], w=[b_eacc[m]])
                                else:
                                    P.op(eng, lam(lambda e, eb, m, c0: e.tensor_tensor(out=eacc[:, m, c0:512], in0=eacc[:, m, c0:512], in1=eb[:, m, c0:512], op=ALU.add), eb, m, c0),
                                         r=[beb, b_eacc[m]], w=[b_eacc[m]])
                            cur = (eb, beb, j, c0)
                            if j == 2:
                                flush()
                        else:
                            cur = None
                        if pend is not None:
                            eb_, beb_, j_, c0_ = pend
                            first = (j_ == 0)
                            last = (j_ == nk - 1)
                            P.op("pe", lam(lambda e, eb_, j_, c0_, h, first, last: e.matmul(pO1[:, c0_:512], lhsT=vv[:, j_, h * 128:(h + 1) * 128], rhs=eb_[:, 0, c0_:512], start=first, stop=last), eb_, j_, c0_, h, first, last),
                                 r=[b_vv, beb_], w=[b_pO1], partial=True)
                            P.op("pe", lam(lambda e, eb_, j_, c0_, h, first, last: e.matmul(pO2[:, c0_:512], lhsT=vv[:, j_, h * 128:(h + 1) * 128], rhs=eb_[:, 1, c0_:512], start=first, stop=last), eb_, j_, c0_, h, first, last),
                                 r=[b_vv, beb_], w=[b_pO2], partial=True)
                        pend = cur
                    obt, bob = ob[it % 2]
                    P.op("pe", lambda e: e.matmul(pR1[:], lhsT=onesf2[:], rhs=eacc[:, 0, :], start=True, stop=True), r=[b_onesf2, b_eacc[0]], w=[b_pR1])
                    P.op("pe", lambda e: e.matmul(pR2[:], lhsT=onesf2[:], rhs=eacc[:, 1, :], start=True, stop=True), r=[b_onesf2, b_eacc[1]], w=[b_pR2])
                    P.op("dve", lambda e: e.tensor_copy(out=o1[:], in_=pO1[:]), r=[b_pO1], w=[b_o1])
                    P.op("dve", lambda e: e.tensor_copy(out=o2[:], in_=pO2[:]), r=[b_pO2], w=[b_o2])
                    P.op("dve", lambda e: e.reciprocal(out=r1[:], in_=pR1[:]), r=[b_pR1], w=[b_r1])
                    P.op("dve", lambda e: e.reciprocal(out=r2[:], in_=pR2[:]), r=[b_pR2], w=[b_r2])
                    P.op("pool", lambda e: e.tensor_tensor(out=o1[:], in0=o1[:], in1=r1[:], op=ALU.mult), r=[b_o1, b_r1], w=[b_o1])
                    P.op("pool", lambda e: e.tensor_tensor(out=o2[:], in0=o2[:], in1=r2[:], op=ALU.mult), r=[b_o2, b_r2], w=[b_o2])
                    P.op("dve", lambda e: e.scalar_tensor_tensor(out=o1[:], in0=o2[:], scalar=nlam[:, 0:1], in1=o1[:], op0=ALU.mult, op1=ALU.add), r=[b_o1, b_o2, b_nlam], w=[b_o1])
                    P.op("pool", lambda e: e.tensor_tensor(out=osq[:], in0=o1[:], in1=o1[:], op=ALU.mult), r=[b_o1], w=[b_osq])

                    def tail(obt=obt, bob=bob, h=h, t=t):
                        P.op("pe", lambda e: e.matmul(pR1[:], lhsT=onesb[:], rhs=osq[:], start=True, stop=True), r=[b_onesb, b_osq], w=[b_pR1])
                        P.op("act", lambda e: e.activation(out=ors[:], in_=pR1[:], func=AF.Ln, scale=1.0 / 128, bias=epss[:, 0:1]), r=[b_pR1, b_epss], w=[b_ors])
                        P.op("act", lambda e: e.activation(out=ors[:], in_=ors[:], func=AF.Exp, scale=-0.5), r=[b_ors], w=[b_ors])
                        P.op("dve", lam(lambda e, obt: e.scalar_tensor_tensor(out=obt[:], in0=o1[:], scalar=gsub[:, 0:1], in1=ors[:], op0=ALU.mult, op1=ALU.mult), obt), r=[b_o1, b_gsub, b_ors], w=[bob])
                        P.op("sp", lam(lambda e, obt, h, t: e.dma_start(out=oT_d[h, :, t * 512:(t + 1) * 512], in_=obt[:]), obt, h, t), r=[bob], dkey="auto")

                    deferred.append(tail)
                    it += 1
            flush()
        P.barrier()

        TC = 256
        NTC = S // TC
        with ExitStack() as pc:
            w_out, b_w_out = mk(pc, "w_out", [128, 8, D], BF16)
            w_up = [mk(pc, f"w_up{kc}", [128, 2 * FF], BF16) for kc in range(8)]
            w_dn, b_w_dn = mk(pc, "w_dn", [128, NF, D], BF16)
            gffn, b_gffn = mk(pc, "gffn", [128, D], F32)
            gfin, b_gfin = mk(pc, "gfin", [128, D], F32)
            wfc, b_wfc = mk(pc, "wfc", [128, NF * 3], F32)
            bfc, b_bfc = mk(pc, "bfc", [128, NF], F32)
            halo, b_halo = mk(pc, "halo", [128, NF, 2], F32)
            mixT, b_mixT = mk(pc, "mixT", [128, 8, TC], BF16)
            xc = [mk(pc, f"xc{i}", [128, 2, D], F32)[0] for i in range(2)]
            bxs = [[Buf(f"xc{i}_{s}") for s in range(2)] for i in range(2)]
            ssc, b_ssc = mk(pc, "ssc", [128, 2], F32)
            rsc, b_rsc = mk(pc, "rsc", [128, 2], F32)
            h2 = [mk(pc, f"h2{i}", [128, D], BF16) for i in range(2)]
            h2T, b_h2T = mk(pc, "h2T", [128, 8, TC], BF16)
            gb = [mk(pc, f"gb{i}", [128, TC + 2], F32) for i in range(2)]
            acc = [mk(pc, f"acc{i}", [128, TC], F32) for i in range(2)]
            sl = [mk(pc, f"sl{i}", [128, TC], F32) for i in range(2)]
            uT, b_uT = mk(pc, "uT", [128, NF, TC], BF16)

            pTc, b_pTc = mkp(pc, "pTc", [128, 8, 128], BF16)
            pY = [mkp(pc, f"pY{i}", [128, 2, 512], F32) for i in range(2)]
            pGV = [mkp(pc, f"pGV{i}", [128, 2, TC], F32) for i in range(2)]

            k_w2 = key("wload2")
            for kc in range(8):
                P.op("pool", lam(lambda e, kc: e.dma_start(out=w_out[:, kc, :], in_=w_out_d[kc * 128:(kc + 1) * 128, :]), kc), w=[b_w_out], partial=True, dkey="auto")
            k_wu = key("wup")
            for kc in range(8):
                wt_, bw_ = w_up[kc]
                P.op("pool", lam(lambda e, wt_, kc: e.dma_start(out=wt_[:], in_=w_up_d[kc * 128:(kc + 1) * 128, :]), wt_, kc), w=[bw_], dkey="auto")
            k_wd = key("wdn")
            for f0 in range(0, NF, 11):
                P.op("pool", lam(lambda e, f0: e.dma_start(out=w_dn[:, f0:f0 + 11, :], in_=w_down_d[f0 * 128:(f0 + 11) * 128, :].rearrange("(f p) n -> p f n", p=128)), f0),
                     w=[b_w_dn], partial=True, dkey="auto")
            P.op("sp", lambda e: e.dma_start(out=gffn[:], in_=gffn_d[0:1, :].partition_broadcast(128)), w=[b_gffn], dkey="auto")
            P.op("sp", lambda e: e.dma_start(out=gfin[:], in_=gfin_d[0:1, :].partition_broadcast(128)), w=[b_gfin], dkey="auto")
            P.op("sp", lambda e: e.dma_start(out=wfc[:], in_=wfc_d[:, :]), w=[b_wfc], dkey="auto")
            P.op("sp", lambda e: e.dma_start(out=bfc[:], in_=bfc_d[:, :]), w=[b_bfc], dkey="auto")
            P.op("pool", lambda e: e.memset(halo[:], 0.0), w=[b_halo])

            k_m = key("mix")
            k_xc = [key("xc0"), key("xc1")]
            k_o = [[key(f"o{i}{s}") for s in range(2)] for i in range(2)]

            def load_x_c(t):
                xb = xc[t % 2]
                s0 = t * TC
                P.op("sp", lam(lambda e, xb, s0: e.dma_start(out=xb[:], in_=x_d[s0:s0 + TC, :].rearrange("(s p) d -> p s d", p=128)), xb, s0), w=bxs[t % 2], dkey="auto")

            def load_mix(t):
                s0 = t * TC
                P.op("sp", lam(lambda e, s0: e.dma_start(out=mixT[:, 0:4, :], in_=aT_d[:, :, s0:s0 + TC].rearrange("c p s -> p c s")), s0), w=[b_mixT], partial=True, dkey="auto")
                P.op("sp", lam(lambda e, s0: e.dma_start(out=mixT[:, 4:8, :], in_=oT_d[:, :, s0:s0 + TC].rearrange("c p s -> p c s")), s0), w=[b_mixT], partial=True, dkey="auto")

            load_x_c(0)
            load_mix(0)
            fctr = 0
            for t in range(NTC):
                xb = xc[t % 2]
                bx = bxs[t % 2]
                if t + 1 < NTC:
                    load_x_c(t + 1)
                for s in range(2):
                    py, bpy = pY[s % 2]
                    ht, bh = h2[s % 2]
                    for half in range(2):
                        for kc in range(8):
                            P.op("pe", lam(lambda e, py, half, kc, s: e.matmul(py[:, half, :], lhsT=mixT[:, kc, s * 128:(s + 1) * 128], rhs=w_out[:, kc, half * 512:(half + 1) * 512], start=(kc == 0), stop=(kc == 7)), py, half, kc, s),
                                 r=[b_mixT, b_w_out], w=[bpy], partial=True)
                    P.op("dve", lam(lambda e, s, py, xb: e.tensor_tensor(out=xb[:, s, :], in0=py[:].rearrange("p a b -> p (a b)"), in1=xb[:, s, :], op=ALU.add), s, py, xb),
                         r=[bpy, bx[s]], w=[bx[s]])
                    P.op("act", lam(lambda e, s, xb, ht: e.activation(out=ht[:], in_=xb[:, s, :], func=AF.Square, scale=1.0 / 32, accum_out=ssc[:, s:s + 1]), s, xb, ht),
                         r=[bx[s]], w=[bh, b_ssc], partial=True)
                if t + 1 < NTC:
                    load_mix(t + 1)
                P.op("act", lambda e: e.activation(out=ssc[:], in_=ssc[:], func=AF.Sqrt, bias=epsr[:, 0:1]), r=[b_ssc, b_epsr], w=[b_ssc])
                P.op("dve", lambda e: e.reciprocal(out=rsc[:], in_=ssc[:]), r=[b_ssc], w=[b_rsc])
                for s in range(2):
                    ht, bh = h2[s % 2]
                    P.op("dve", lam(lambda e, ht, s, xb: e.scalar_tensor_tensor(out=ht[:], in0=xb[:, s, :], scalar=rsc[:, s:s + 1], in1=gffn[:], op0=ALU.mult, op1=ALU.mult), ht, s, xb),
                         r=[bx[s], b_rsc, b_gffn], w=[bh])
                    for c in range(8):
                        P.op("pe", lam(lambda e, ht, c: e.transpose(out=pTc[:, c, :], in_=ht[:, c * 128:(c + 1) * 128], identity=ident[:]), ht, c), r=[bh, b_ident], w=[b_pTc], partial=True)
                    P.op("act", lam(lambda e, s: e.copy(out=h2T[:, :, s * 128:(s + 1) * 128], in_=pTc[:]), s), r=[b_pTc], w=[b_h2T], partial=True)
                for f in range(NF):
                    i = fctr % 2
                    fctr += 1
                    pgv, bpgv = pGV[i]
                    gbt, bgb = gb[i]
                    at, bat = acc[i]
                    st, bst = sl[i]
                    for kc in range(8):
                        wt_, bw_ = w_up[kc]
                        P.op("pe", lam(lambda e, pgv, wt_, kc, f: e.matmul(pgv[:, 0, :], lhsT=wt_[:, f * 128:(f + 1) * 128], rhs=h2T[:, kc, :], start=(kc == 0), stop=(kc == 7)), pgv, wt_, kc, f),
                             r=[bw_, b_h2T], w=[bpgv], partial=True)
                    for kc in range(8):
                        wt_, bw_ = w_up[kc]
                        P.op("pe", lam(lambda e, pgv, wt_, kc, f: e.matmul(pgv[:, 1, :], lhsT=wt_[:, FF + f * 128:FF + (f + 1) * 128], rhs=h2T[:, kc, :], start=(kc == 0), stop=(kc == 7)), pgv, wt_, kc, f),
                             r=[bw_, b_h2T], w=[bpgv], partial=True)
                    P.op("pool", lam(lambda e, gbt, f: e.tensor_copy(out=gbt[:, 0:2], in_=halo[:, f, :]), gbt, f), r=[b_halo], w=[bgb], partial=True)
                    P.op("act", lam(lambda e, gbt, pgv: e.copy(out=gbt[:, 2:TC + 2], in_=pgv[:, 0, :]), gbt, pgv), r=[bpgv], w=[bgb], partial=True)
                    P.op("pool", lam(lambda e, gbt, f: e.tensor_copy(out=halo[:, f, :], in_=gbt[:, TC:TC + 2]), gbt, f), r=[bgb], w=[b_halo], partial=True)
                    P.op("dve", lam(lambda e, at, gbt, f: e.tensor_scalar(out=at[:], in0=gbt[:, 0:TC], scalar1=wfc[:, 3 * f:3 * f + 1], scalar2=bfc[:, f:f + 1], op0=ALU.mult, op1=ALU.add), at, gbt, f),
                         r=[bgb, b_wfc, b_bfc], w=[bat])
                    P.op("dve", lam(lambda e, at, gbt, f: e.scalar_tensor_tensor(out=at[:], in0=gbt[:, 1:TC + 1], scalar=wfc[:, 3 * f + 1:3 * f + 2], in1=at[:], op0=ALU.mult, op1=ALU.add), at, gbt, f),
                         r=[bgb, bat], w=[bat])
                    P.op("dve", lam(lambda e, at, gbt, f: e.scalar_tensor_tensor(out=at[:], in0=gbt[:, 2:TC + 2], scalar=wfc[:, 3 * f + 2:3 * f + 3], in1=at[:], op0=ALU.mult, op1=ALU.add), at, gbt, f),
                         r=[bgb, bat], w=[bat])
                    P.op("act", lam(lambda e, st, at: e.activation(out=st[:], in_=at[:], func=AF.Silu), st, at), r=[bat], w=[bst])
                    P.op("dve", lam(lambda e, f, st, pgv: e.tensor_tensor(out=uT[:, f, :], in0=st[:], in1=pgv[:, 1, :], op=ALU.mult), f, st, pgv), r=[bst, bpgv], w=[b_uT], partial=True)
                for s in range(2):
                    py, bpy = pY[s % 2]
                    ht, bh = h2[s % 2]
                    for half in range(2):
                        for f in range(NF):
                            P.op("pe", lam(lambda e, py, half, f, s: e.matmul(py[:, half, :], lhsT=uT[:, f, s * 128:(s + 1) * 128], rhs=w_dn[:, f, half * 512:(half + 1) * 512], start=(f == 0), stop=(f == NF - 1)), py, half, f, s),
                                 r=[b_uT, b_w_dn], w=[bpy], partial=True)
                    P.op("dve", lam(lambda e, s, py, xb: e.tensor_tensor(out=xb[:, s, :], in0=py[:].rearrange("p a b -> p (a b)"), in1=xb[:, s, :], op=ALU.add), s, py, xb),
                         r=[bpy, bx[s]], w=[bx[s]])
                    P.op("act", lam(lambda e, s, xb, ht: e.activation(out=ht[:], in_=xb[:, s, :], func=AF.Square, scale=1.0 / 32, accum_out=ssc[:, s:s + 1]), s, xb, ht),
                         r=[bx[s]], w=[bh, b_ssc], partial=True)
                P.op("act", lambda e: e.activation(out=ssc[:], in_=ssc[:], func=AF.Sqrt, bias=epsr[:, 0:1]), r=[b_ssc, b_epsr], w=[b_ssc])
                P.op("dve", lambda e: e.reciprocal(out=rsc[:], in_=ssc[:]), r=[b_ssc], w=[b_rsc])
                for s in range(2):
                    P.op("dve", lam(lambda e, xb, s: e.scalar_tensor_tensor(out=xb[:, s, :], in0=xb[:, s, :], scalar=rsc[:, s:s + 1], in1=gfin[:], op0=ALU.mult, op1=ALU.mult), xb, s),
                         r=[bx[s], b_rsc, b_gfin], w=[bx[s]])
                    P.op("sp", lam(lambda e, xb, t, s: e.dma_start(out=out_d[t * TC + s * 128:t * TC + (s + 1) * 128, :], in_=xb[:, s, :]), xb, t, s), r=[bx[s]], dkey="auto")
        P.barrier()
        P.op("sp", lambda e: e.nop())

        engsem = {e: top.enter_context(nc.semaphore("s_" + e)) for e in ("act", "dve", "pool", "pe")}
        dmasem = {k: top.enter_context(nc.semaphore("d_" + k)) for k in P.keys}
        run = P.emitter(engsem, dmasem)
        with nc.Block() as block:
            @block.sync
            def _(e):
                run("sp", e)

            @block.scalar
            def _(e):
                run("act", e)

            @block.vector
            def _(e):
                run("dve", e)

            @block.gpsimd
            def _(e):
                run("pool", e)

            @block.tensor
            def _(e):
                run("pe", e)
    return nc


def host_shared(inp, S):
    f32 = np.float32
    w_in = np.asarray(inp["w_in"], f32)[0]
    cv, cg, q, k, v = np.split(w_in, [512, 1024, 1536, 2048], axis=1)

    def swap(w):
        w4 = w.reshape(D, 4, 2, 64)
        return np.concatenate([w4[..., 32:], w4[..., :32]], axis=-1).reshape(D, 512)

    w_in_p = np.ascontiguousarray(np.concatenate([cv, cg, q, swap(q), k, swap(k), v], axis=1))
    inv_freq = (1.0 / (np.float32(10000.0) ** (np.arange(0, 64, 2, dtype=f32) / np.float32(64)))).astype(f32)
    ang = (np.arange(S, dtype=f32)[:, None] * inv_freq[None, :]).astype(f32)
    cos = np.cos(ang.astype(np.float64)).astype(f32).T
    sin = np.sin(ang.astype(np.float64)).astype(f32).T
    cosT = np.ascontiguousarray(np.concatenate([cos, cos, cos, cos], axis=0))
    sinT = np.ascontiguousarray(np.concatenate([-sin, sin, -sin, sin], axis=0))

    def col4(vv):
        return np.asarray(vv, f32).reshape(4, 128).T

    cvec = np.ascontiguousarray(np.concatenate([col4(inp["b_dw"][0]), col4(inp["ln_g"][0]), col4(inp["ln_b"][0])], axis=1))
    lamv = np.ascontiguousarray(np.concatenate([np.asarray(inp[n], f32)[0] for n in ("lambda_q1", "lambda_k1", "lambda_q2", "lambda_k2")])[None, :])
    wfc = np.ascontiguousarray(np.asarray(inp["w_fconv"], f32)[0].T.reshape(NF, 128, 3).transpose(1, 0, 2).reshape(128, NF * 3))
    bfc = np.ascontiguousarray(np.asarray(inp["b_fconv"], f32)[0].reshape(NF, 128).T)
    return {
        "w_in_p": w_in_p,
        "w_out": np.ascontiguousarray(np.asarray(inp["w_out"], f32)[0]),
        "w_up": np.ascontiguousarray(np.asarray(inp["w_up"], f32)[0]),
        "w_down": np.ascontiguousarray(np.asarray(inp["w_down"], f32)[0]),
        "cosT": cosT,
        "sinT": sinT,
        "ident": np.eye(128, dtype=f32),
        "g_mix": np.ascontiguousarray(np.asarray(inp["g_mix"], f32)[0][None, :]),
        "g_ffn": np.ascontiguousarray(np.asarray(inp["g_ffn"], f32)[0][None, :]),
        "g_final": np.ascontiguousarray(np.asarray(inp["g_final"], f32)[None, :]),
        "wdw_t": np.ascontiguousarray(np.asarray(inp["w_dw"], f32)[0].T),
        "cvec": cvec,
        "lamv": lamv,
        "g_subln": np.ascontiguousarray(np.asarray(inp["g_subln"], f32)[0][:, None]),
        "wfc": wfc,
        "bfc": bfc,
    }


def kernel(**inputs):
    x = np.asarray(inputs["x"], np.float32)
    B, S, _ = x.shape
    shared = host_shared(inputs, S)
    nc = build_nc(S)
    in_maps = []
    for b in range(B):
        m = dict(shared)
        m["x"] = np.ascontiguousarray(x[b])
        in_maps.append(m)
    res = run_bass_kernel_spmd(nc, in_maps, core_ids=list(range(B)))
    return np.stack([np.asarray(r["out"], np.float32) for r in res.results], axis=0)
```

```python
import math
from contextlib import ExitStack

import numpy as np
import concourse.bass as bass
import concourse.mybir as mybir
from concourse.bass_utils import run_bass_kernel_spmd

F32 = mybir.dt.float32
BF16 = mybir.dt.bfloat16
AF = mybir.ActivationFunctionType
ALU = mybir.AluOpType
AX = mybir.AxisListType

D = 1024
SEQ = 8192
NCORES = 8
FF = 2816
NF = FF // 128
INP = 3584
RMS_EPS = 1e-6
LN_EPS = 1e-5
SUBLN_EPS = 1e-5
LAM_INIT = 0.8 - 0.6 * math.exp(-0.3 * 0)


class Buf:
    __slots__ = ("name", "writers", "reads", "war")

    def __init__(self, name=""):
        self.name = name
        self.writers = []
        self.reads = []
        self.war = []


class Op:
    __slots__ = ("eng", "fn", "deps", "sig", "dkey", "dval", "needed")


class Prog:
    ENGS = ("sp", "act", "dve", "pool", "pe")

    def __init__(self):
        self.streams = {e: [] for e in self.ENGS}
        self.dma_counts = {}
        self.last_dma = {}
        self.bar = []
        self.keys = []

    def op(self, eng, fn, r=(), w=(), partial=False, dkey=None, deps=()):
        o = Op()
        o.eng = eng
        o.fn = fn
        o.sig = None
        o.needed = False
        if dkey == "auto":
            dkey = ("kl_" + w[0].name) if w else ("ks_" + r[0].name)
            if dkey not in self.dma_counts:
                self.keys.append(dkey)
        o.dkey = dkey
        o.dval = None
        d = list(self.bar)
        d.extend(x for x in deps if x is not None)
        for b in r:
            d.extend(b.writers)
        for b in w:
            if b.reads:
                b.war = b.reads
                b.reads = []
                b.writers = []
            elif not partial:
                d.extend(x for x in b.writers if x.eng != eng or x.dkey is not None)
                b.writers = []
            d.extend(x for x in b.war if x.eng != eng or x.dkey is not None)
            b.writers.append(o)
        for b in r:
            b.reads.append(o)
        o.deps = d
        if dkey is not None:
            self.dma_counts[dkey] = self.dma_counts.get(dkey, 0) + 1
            o.dval = 16 * self.dma_counts[dkey]
            self.last_dma[dkey] = o
        self.streams[eng].append(o)
        return o

    def barrier(self):
        b = []
        for e in self.ENGS:
            if e == "sp":
                continue
            for o in reversed(self.streams[e]):
                if o.dkey is None:
                    b.append(o)
                    break
        b.extend(self.last_dma.values())
        self.bar = b

    def emitter(self, engsem, dmasem):
        for e in self.ENGS:
            for o in self.streams[e]:
                for d in o.deps:
                    d.needed = True
        for e in self.ENGS:
            c = 0
            for o in self.streams[e]:
                if o.dkey is None and o.needed:
                    c += 1
                    o.sig = c

        def run(ename, e):
            waited = {}
            for o in self.streams[ename]:
                need = {}
                for d in o.deps:
                    if d.dkey is not None:
                        k = ("d", d.dkey)
                        v = d.dval
                    else:
                        k = ("e", d.eng)
                        v = d.sig
                    if need.get(k, 0) < v:
                        need[k] = v
                for k, v in need.items():
                    if waited.get(k, 0) < v:
                        sem = dmasem[k[1]] if k[0] == "d" else engsem[k[1]]
                        e.wait_ge(sem, v)
                        waited[k] = v
                ins = o.fn(e)
                if o.dkey is not None:
                    ins.then_inc(dmasem[o.dkey], 16)
                elif o.sig is not None:
                    ins.then_inc(engsem[ename], 1)

        return run


def build_nc(S=SEQ, dbg=False):
    nc = bass.Bass("TRN2", target_bir_lowering=False)
    NT = S // 512
    NCH = S // 128

    def din(name, shape):
        return nc.dram_tensor(name, shape, F32, kind="ExternalInput").ap()

    x_d = din("x", [S, D])
    w_in_d = din("w_in_p", [D, INP])
    w_out_d = din("w_out", [D, D])
    w_up_d = din("w_up", [D, 2 * FF])
    w_down_d = din("w_down", [FF, D])
    cos_d = din("cosT", [128, S])
    sin_d = din("sinT", [128, S])
    ident_d = din("ident", [128, 128])
    gmix_d = din("g_mix", [1, D])
    gffn_d = din("g_ffn", [1, D])
    gfin_d = din("g_final", [1, D])
    wdw_d = din("wdw_t", [512, 31])
    cvec_d = din("cvec", [128, 12])
    lamv_d = din("lamv", [1, 256])
    gsub_d = din("g_subln", [128, 1])
    wfc_d = din("wfc", [128, NF * 3])
    bfc_d = din("bfc", [128, NF])
    out_d = nc.dram_tensor("out", [S, D], F32, kind="ExternalOutput").ap()

    okind = "ExternalOutput" if dbg else "Internal"
    aT_d = nc.dram_tensor("aT_s", [4, 128, S], BF16, kind=okind).ap()
    qT_d = nc.dram_tensor("qT_s", [4, 128, S], BF16, kind=okind).ap()
    kT_d = nc.dram_tensor("kT_s", [4, 128, S], BF16, kind=okind).ap()
    v_d = nc.dram_tensor("v_s", [NCH, 128, 512], BF16, kind=okind).ap()
    oT_d = nc.dram_tensor("oT_s", [4, 128, S], BF16, kind=okind).ap()

    P = Prog()
    sem_keys = []

    def key(name):
        sem_keys.append(name)
        return name

    def lam(f, *a):
        return lambda e: f(e, *a)

    with ExitStack() as top:
        def mk(es, name, shape, dt):
            return es.enter_context(nc.sbuf_tensor("sb_" + name, shape, dt)), Buf(name)

        def mkp(es, name, shape, dt):
            return es.enter_context(nc.psum_tensor("ps_" + name, shape, dt)), Buf(name)

        ident, b_ident = mk(top, "ident", [128, 128], BF16)
        onesb, b_onesb = mk(top, "onesb", [128, 128], BF16)
        epsr, b_epsr = mk(top, "epsr", [128, 1], F32)
        epsl, b_epsl = mk(top, "epsl", [128, 1], F32)
        cvec, b_cvec = mk(top, "cvec", [128, 12], F32)
        gsub, b_gsub = mk(top, "gsub", [128, 1], F32)
        lams, b_lams = mk(top, "lams", [128, 2], F32)
        nlam, b_nlam = mk(top, "nlam", [128, 1], F32)

        with ExitStack() as pa:
            w_in, b_w_in = mk(pa, "w_in", [128, 8, INP], BF16)
            gmix, b_gmix = mk(pa, "gmix", [128, D], F32)
            wdw, b_wdw = mk(pa, "wdw", [128, 4, 31], F32)
            xt = [mk(pa, f"xt{i}", [128, 4, D], F32) for i in range(2)]
            junk, b_junk = mk(pa, "junk", [128, D], BF16)
            ss, b_ss = mk(pa, "ss", [128, 4], F32)
            rstd, b_rstd = mk(pa, "rstd", [128, 4], F32)
            hb = [mk(pa, f"hb{i}", [128, D], BF16) for i in range(2)]
            hT, b_hT = mk(pa, "hT", [128, 8, 512], BF16)
            glu, b_glu = mk(pa, "glu", [128, 4, 542], BF16)
            diag, b_diag = mk(pa, "diag", [128, 4 * 31, 128], BF16)
            th = [mk(pa, f"th{i}", [128, 512], F32) for i in range(2)]
            yc, _b = mk(pa, "yc", [128, 4, 512], F32)
            b_ycs = [Buf(f"yc{c}") for c in range(4)]
            ysq = [mk(pa, f"ysq{i}", [128, 512], F32) for i in range(2)]
            mean, b_mean = mk(pa, "mean", [128, 512], F32)
            m2, b_m2 = mk(pa, "m2", [128, 512], F32)
            lrs, b_lrs = mk(pa, "lrs", [128, 512], F32)
            zt = [mk(pa, f"zt{i}", [128, 512], F32) for i in range(2)]
            aout, b_aout = mk(pa, "aout", [128, 4, 512], BF16)
            cst = [mk(pa, f"cst{i}", [128, 2, 512], F32) for i in range(2)]
            t1 = [mk(pa, f"t1{i}", [128, 512], F32) for i in range(2)]
            t2 = [mk(pa, f"t2{i}", [128, 512], F32) for i in range(2)]
            qout, b_qout = mk(pa, "qout", [128, 4, 512], BF16)
            kout, b_kout = mk(pa, "kout", [128, 4, 512], BF16)
            vout, b_vout = mk(pa, "vout", [128, 4, 512], BF16)

            pT, b_pT = mkp(pa, "pT", [128, 8, 128], BF16)
            pA = [mkp(pa, f"pA{i}", [128, 512], F32) for i in range(2)]
            pB = [mkp(pa, f"pB{i}", [128, 512], F32) for i in range(2)]
            pS1, b_pS1 = mkp(pa, "pS1", [128, 512], F32)
            pS2, b_pS2 = mkp(pa, "pS2", [128, 512], F32)

            identf, b_identf = mk(pa, "identf", [128, 128], F32)
            onesf, b_onesf = mk(pa, "onesf", [128, 128], F32)
            lamv, b_lamv = mk(pa, "lamv", [128, 256], F32)
            lamp, b_lamp = mk(pa, "lamp", [128, 128], F32)
            k_c = key("const")
            P.op("sp", lambda e: e.dma_start(out=identf[:], in_=ident_d[:, :]), w=[b_identf], dkey="auto")
            P.op("sp", lambda e: e.dma_start(out=cvec[:], in_=cvec_d[:, :]), w=[b_cvec], dkey="auto")
            P.op("sp", lambda e: e.dma_start(out=gsub[:], in_=gsub_d[:, :]), w=[b_gsub], dkey="auto")
            P.op("sp", lambda e: e.dma_start(out=lamv[:], in_=lamv_d[0:1, :].partition_broadcast(128)), w=[b_lamv], dkey="auto")
            P.op("dve", lambda e: e.tensor_copy(out=ident[:], in_=identf[:]), r=[b_identf], w=[b_ident])
            P.op("pool", lambda e: e.memset(onesf[:], 1.0), w=[b_onesf])
            P.op("pool", lambda e: e.memset(onesb[:], 1.0), w=[b_onesb])
            P.op("pool", lambda e: e.memset(epsr[:], RMS_EPS), w=[b_epsr])
            P.op("pool", lambda e: e.memset(epsl[:], LN_EPS), w=[b_epsl])
            P.op("dve", lambda e: e.tensor_tensor(out=lamp[:, 0:64], in0=lamv[:, 0:64], in1=lamv[:, 64:128], op=ALU.mult), r=[b_lamv], w=[b_lamp], partial=True)
            P.op("dve", lambda e: e.tensor_tensor(out=lamp[:, 64:128], in0=lamv[:, 128:192], in1=lamv[:, 192:256], op=ALU.mult), r=[b_lamv], w=[b_lamp], partial=True)
            P.op("dve", lambda e: e.reduce_sum(out=lams[:, 0:2], in_=lamp[:].rearrange("p (a b) -> p a b", a=2), axis=AX.X), r=[b_lamp], w=[b_lams])
            P.op("act", lambda e: e.activation(out=lams[:], in_=lams[:], func=AF.Exp), r=[b_lams], w=[b_lams])
            P.op("dve", lambda e: e.tensor_tensor(out=nlam[:], in0=lams[:, 1:2], in1=lams[:, 0:1], op=ALU.subtract), r=[b_lams], w=[b_nlam])
            P.op("dve", lambda e: e.tensor_scalar(out=nlam[:], in0=nlam[:], scalar1=-LAM_INIT, scalar2=None, op0=ALU.add), r=[b_nlam], w=[b_nlam])
            P.op("dve", lambda e: e.tensor_scalar(out=gsub[:], in0=gsub[:], scalar1=1.0 - LAM_INIT, scalar2=None, op0=ALU.mult), r=[b_gsub], w=[b_gsub])

            k_w = key("wload")
            for kc in range(8):
                P.op("pool", lam(lambda e, kc: e.dma_start(out=w_in[:, kc, :], in_=w_in_d[kc * 128:(kc + 1) * 128, :]), kc),
                     w=[b_w_in], partial=True, dkey="auto")
            P.op("sp", lambda e: e.dma_start(out=gmix[:], in_=gmix_d[0:1, :].partition_broadcast(128)), w=[b_gmix], dkey="auto")
            P.op("sp", lambda e: e.dma_start(out=wdw[:], in_=wdw_d.rearrange("(c p) k -> p c k", p=128)), w=[b_wdw], dkey="auto")
            P.op("dve", lambda e: e.tensor_scalar(out=wdw[:], in0=wdw[:], scalar1=0.5, scalar2=None, op0=ALU.mult), r=[b_wdw], w=[b_wdw])
            P.op("pool", lambda e: e.memset(glu[:, :, 0:30], 0.0), w=[b_glu])
            for c in range(4):
                for k in range(31):
                    P.op("dve", lam(lambda e, c, k: e.tensor_scalar(out=diag[:, c * 31 + k, :], in0=identf[:], scalar1=wdw[:, c, k:k + 1], scalar2=None, op0=ALU.mult), c, k),
                         r=[b_identf, b_wdw], w=[b_diag], partial=True)

            k_x = [key("xa0"), key("xa1")]
            k_cs = [key("cs0"), key("cs1")]
            k_oa = key("outA")
            pair_ctr = [0]

            def load_x(t):
                xb, bx = xt[t % 2]
                P.op("sp", lam(lambda e, xb, t: e.dma_start(out=xb[:], in_=x_d[t * 512:(t + 1) * 512, :].rearrange("(s p) d -> p s d", p=128)), xb, t),
                     w=[bx], dkey="auto")

            def load_cs(t):
                cb, bc = cst[t % 2]
                P.op("sp", lam(lambda e, cb, t: e.dma_start(out=cb[:, 0, :], in_=cos_d[:, t * 512:(t + 1) * 512]), cb, t), w=[bc], partial=True, dkey="auto")
                P.op("sp", lam(lambda e, cb, t: e.dma_start(out=cb[:, 1, :], in_=sin_d[:, t * 512:(t + 1) * 512]), cb, t), w=[bc], partial=True, dkey="auto")

            load_x(0)
            load_cs(0)
            for t in range(NT):
                xb, bx = xt[t % 2]
                cb, bc = cst[t % 2]
                if t + 1 < NT:
                    load_x(t + 1)
                    load_cs(t + 1)
                for s in range(4):
                    P.op("act", lam(lambda e, xb, s: e.activation(out=junk[:], in_=xb[:, s, :], func=AF.Square, scale=1.0 / 32, accum_out=ss[:, s:s + 1]), xb, s),
                         r=[bx], w=[b_junk, b_ss], partial=True)
                P.op("act", lambda e: e.activation(out=ss[:], in_=ss[:], func=AF.Sqrt, bias=epsr[:, 0:1]), r=[b_ss, b_epsr], w=[b_ss])
                P.op("dve", lambda e: e.reciprocal(out=rstd[:], in_=ss[:]), r=[b_ss], w=[b_rstd])
                for s in range(4):
                    hbt, bh = hb[s % 2]
                    P.op("dve", lam(lambda e, hbt, xb, s: e.scalar_tensor_tensor(out=hbt[:], in0=xb[:, s, :], scalar=rstd[:, s:s + 1], in1=gmix[:], op0=ALU.mult, op1=ALU.mult), hbt, xb, s),
                         r=[bx, b_rstd, b_gmix], w=[bh])
                    for c in range(8):
                        P.op("pe", lam(lambda e, hbt, c: e.transpose(out=pT[:, c, :], in_=hbt[:, c * 128:(c + 1) * 128], identity=ident[:]), hbt, c),
                             r=[bh, b_ident], w=[b_pT], partial=True)
                    P.op("act", lam(lambda e, s: e.copy(out=hT[:, :, s * 128:(s + 1) * 128], in_=pT[:]), s), r=[b_pT], w=[b_hT], partial=True)

                def mm_fm(pt, bp, col0):
                    for kc in range(8):
                        P.op("pe", lam(lambda e, pt, kc, col0: e.matmul(pt[:], lhsT=w_in[:, kc, col0:col0 + 128], rhs=hT[:, kc, :], start=(kc == 0), stop=(kc == 7)), pt, kc, col0),
                             r=[b_w_in, b_hT], w=[bp], partial=True)

                for c in range(4):
                    i = pair_ctr[0] % 2
                    pair_ctr[0] += 1
                    (pa_, bpa), (pb_, bpb) = pA[i], pB[i]
                    tht, bth = th[c % 2]
                    mm_fm(pa_, bpa, c * 128)
                    mm_fm(pb_, bpb, 512 + c * 128)
                    P.op("act", lam(lambda e, tht, pb_: e.activation(out=tht[:], in_=pb_[:], func=AF.Tanh, scale=0.5), tht, pb_), r=[bpb], w=[bth])
                    P.op("dve", lam(lambda e, c, tht, pa_: e.scalar_tensor_tensor(out=glu[:, c, 30:542], in0=tht[:], scalar=1.0, in1=pa_[:], op0=ALU.add, op1=ALU.mult), c, tht, pa_),
                         r=[bth, bpa], w=[b_glu], partial=True)
                for which, col0, ot, bo in ((0, 1024, qout, b_qout), (1, 2048, kout, b_kout)):
                    for h in range(4):
                        i = pair_ctr[0] % 2
                        pair_ctr[0] += 1
                        (pa_, bpa), (pb_, bpb) = pA[i], pB[i]
                        t1t, bt1 = t1[h % 2]
                        t2t, bt2 = t2[h % 2]
                        mm_fm(pa_, bpa, col0 + h * 128)
                        mm_fm(pb_, bpb, col0 + 512 + h * 128)
                        P.op("dve", lam(lambda e, t1t, pa_, cb: e.tensor_tensor(out=t1t[:], in0=pa_[:], in1=cb[:, 0, :], op=ALU.mult), t1t, pa_, cb), r=[bpa, bc], w=[bt1])
                        P.op("dve", lam(lambda e, t2t, pb_, cb: e.tensor_tensor(out=t2t[:], in0=pb_[:], in1=cb[:, 1, :], op=ALU.mult), t2t, pb_, cb), r=[bpb, bc], w=[bt2])
                        P.op("pool", lam(lambda e, ot, h, t1t, t2t: e.tensor_tensor(out=ot[:, h, :], in0=t1t[:], in1=t2t[:], op=ALU.add), ot, h, t1t, t2t),
                             r=[bt1, bt2], w=[bo], partial=True)
                for s in range(4):
                    i = pair_ctr[0] % 2
                    pair_ctr[0] += 1
                    pa_, bpa = pA[i]
                    for kc in range(8):
                        P.op("pe", lam(lambda e, pa_, kc, s: e.matmul(pa_[:], lhsT=hT[:, kc, s * 128:(s + 1) * 128], rhs=w_in[:, kc, 3072:3584], start=(kc == 0), stop=(kc == 7)), pa_, kc, s),
                             r=[b_w_in, b_hT], w=[bpa], partial=True)
                    P.op("act", lam(lambda e, s, pa_: e.copy(out=vout[:, s, :], in_=pa_[:]), s, pa_), r=[bpa], w=[b_vout], partial=True)
                P.op("sp", lam(lambda e, t: e.dma_start(out=qT_d[:, :, t * 512:(t + 1) * 512].rearrange("h p s -> p h s"), in_=qout[:]), t), r=[b_qout], dkey="auto")
                P.op("sp", lam(lambda e, t: e.dma_start(out=kT_d[:, :, t * 512:(t + 1) * 512].rearrange("h p s -> p h s"), in_=kout[:]), t), r=[b_kout], dkey="auto")
                P.op("sp", lam(lambda e, t: e.dma_start(out=v_d[t * 4:(t + 1) * 4, :, :].rearrange("c p e -> p c e"), in_=vout[:]), t), r=[b_vout], dkey="auto")
                for c in range(4):
                    pc_, bpc = (pA[c // 2] if c % 2 == 0 else pB[c // 2])
                    for k in range(31):
                        P.op("pe", lam(lambda e, pc_, c, k: e.matmul(pc_[:], lhsT=diag[:, c * 31 + k, :], rhs=glu[:, c, k:k + 512], start=(k == 0), stop=(k == 30)), pc_, c, k),
                             r=[b_diag, b_glu], w=[bpc], partial=True)
                    P.op("dve", lam(lambda e, pc_, c: e.tensor_scalar(out=yc[:, c, :], in0=pc_[:], scalar1=cvec[:, c:c + 1], scalar2=None, op0=ALU.add), pc_, c),
                         r=[bpc, b_cvec], w=[b_ycs[c]])
                P.op("pool", lambda e: e.tensor_copy(out=glu[:, :, 0:30], in_=glu[:, :, 512:542]), r=[b_glu], w=[b_glu])
                for c in range(4):
                    yq, byq = ysq[c % 2]
                    P.op("act", lam(lambda e, yq, c: e.activation(out=yq[:], in_=yc[:, c, :], func=AF.Square), yq, c), r=[b_ycs[c]], w=[byq])
                    P.op("pe", lam(lambda e, c: e.matmul(pS1[:], lhsT=onesf[:], rhs=yc[:, c, :], start=(c == 0), stop=(c == 3)), c), r=[b_onesf, b_ycs[c]], w=[b_pS1], partial=True)
                    P.op("pe", lam(lambda e, c, yq: e.matmul(pS2[:], lhsT=onesf[:], rhs=yq[:], start=(c == 0), stop=(c == 3)), c, yq), r=[b_onesf, byq], w=[b_pS2], partial=True)
                P.op("dve", lambda e: e.tensor_scalar(out=mean[:], in0=pS1[:], scalar1=1.0 / 512, scalar2=None, op0=ALU.mult), r=[b_pS1], w=[b_mean])
                P.op("dve", lambda e: e.tensor_tensor(out=m2[:], in0=mean[:], in1=mean[:], op=ALU.mult), r=[b_mean], w=[b_m2])
                P.op("dve", lambda e: e.scalar_tensor_tensor(out=m2[:], in0=pS2[:], scalar=1.0 / 512, in1=m2[:], op0=ALU.mult, op1=ALU.subtract), r=[b_pS2, b_m2], w=[b_m2])
                P.op("act", lambda e: e.activation(out=m2[:], in_=m2[:], func=AF.Sqrt, bias=epsl[:, 0:1]), r=[b_m2, b_epsl], w=[b_m2])
                P.op("dve", lambda e: e.reciprocal(out=lrs[:], in_=m2[:]), r=[b_m2], w=[b_lrs])
                for c in range(4):
                    z, bz = zt[c % 2]
                    P.op("dve", lam(lambda e, z, c: e.tensor_tensor(out=z[:], in0=yc[:, c, :], in1=mean[:], op=ALU.subtract), z, c), r=[b_ycs[c], b_mean], w=[bz])
                    P.op("pool", lam(lambda e, z: e.tensor_tensor(out=z[:], in0=z[:], in1=lrs[:], op=ALU.mult), z), r=[bz, b_lrs], w=[bz])
                    P.op("act", lam(lambda e, z, c: e.activation(out=aout[:, c, :], in_=z[:], func=AF.Silu, scale=cvec[:, 4 + c:5 + c], bias=cvec[:, 8 + c:9 + c]), z, c),
                         r=[bz, b_cvec], w=[b_aout], partial=True)
                P.op("sp", lam(lambda e, t: e.dma_start(out=aT_d[:, :, t * 512:(t + 1) * 512].rearrange("c p s -> p c s"), in_=aout[:]), t), r=[b_aout], dkey="auto")
        P.barrier()

        with ExitStack() as pb:
            kT, b_kT = mk(pb, "kT", [128, 4, S], BF16)
            vv, b_vv = mk(pb, "vv", [128, NCH, 512], BF16)
            qt = [mk(pb, f"qt{i}", [128, 512], BF16) for i in range(2)]
            NE = 4
            et = [mk(pb, f"et{i}", [128, 2, 512], BF16) for i in range(NE)]
            gA = [mk(pb, f"gA{i}", [128, 2, 512], BF16) for i in range(2)]
            gB = [mk(pb, f"gB{i}", [128, 2, 512], BF16) for i in range(2)]
            r1, b_r1 = mk(pb, "r1", [128, 512], F32)
            r2, b_r2 = mk(pb, "r2", [128, 512], F32)
            o1, b_o1 = mk(pb, "o1", [128, 512], F32)
            o2, b_o2 = mk(pb, "o2", [128, 512], F32)
            osq, b_osq = mk(pb, "osq", [128, 512], BF16)
            ors, b_ors = mk(pb, "ors", [128, 512], F32)
            ob = [mk(pb, f"ob{i}", [128, 512], BF16) for i in range(2)]
            epss, b_epss = mk(pb, "epss", [128, 1], F32)

            pSc = [mkp(pb, f"pSc{i}", [128, 2, 512], F32) for i in range(2)]
            pO1, b_pO1 = mkp(pb, "pO1", [128, 512], F32)
            pO2, b_pO2 = mkp(pb, "pO2", [128, 512], F32)
            pR1, b_pR1 = mkp(pb, "pR1", [128, 512], F32)
            pR2, b_pR2 = mkp(pb, "pR2", [128, 512], F32)

            P.op("pool", lambda e: e.memset(epss[:], SUBLN_EPS), w=[b_epss])
            for h in range(4):
                P.op("sp", lam(lambda e, h: e.dma_start(out=kT[:, h, :], in_=kT_d[h, :, :]), h), w=[b_kT], partial=True, dkey="auto")
            VG = 8
            for c0 in range(0, NCH, VG):
                P.op("sp", lam(lambda e, c0: e.dma_start(out=vv[:, c0:c0 + VG, :], in_=v_d[c0:c0 + VG, :, :].rearrange("c p e -> p c e")), c0),
                     w=[b_vv], partial=True, dkey="auto")
            scale = 64 ** -0.5
            it = 0
            ectr = 0
            sctr = 0
            deferred = []

            def flush():
                for f in deferred:
                    f()
                deferred.clear()

            for h in range(4):
                for t in range(NT):
                    qb, bq = qt[it % 2]
                    P.op("sp", lam(lambda e, qb, h, t: e.dma_start(out=qb[:], in_=qT_d[h, :, t * 512:(t + 1) * 512]), qb, h, t), w=[bq], dkey="auto")
                    nk = 4 * (t + 1)
                    pend = None
                    pend_sums = []
                    prev_e = None
                    for j in range(nk + 1):
                        if j < nk:
                            jd = j - 4 * t
                            c0 = 128 * jd if jd > 0 else 0
                            ps, bps = pSc[sctr % 2]
                            sctr += 1
                            eb, beb = et[ectr % NE]
                            ectr += 1
                            for m in range(2):
                                P.op("pe", lam(lambda e, ps, m, h, j, qb, c0: e.matmul(ps[:, m, c0:512], lhsT=kT[m * 64:(m + 1) * 64, h, j * 128:(j + 1) * 128], rhs=qb[m * 64:(m + 1) * 64, c0:512], start=True, stop=True), ps, m, h, j, qb, c0),
                                     r=[b_kT, bq], w=[bps], partial=True)
                            P.op("act", lam(lambda e, eb, ps, c0: e.activation(out=eb[:, :, c0:512], in_=ps[:, :, c0:512], func=AF.Exp, scale=scale), eb, ps, c0), r=[bps], w=[beb])
                            if jd >= 0:
                                P.op("pool", lam(lambda e, eb, c0: e.memset(eb[64:128, :, c0:c0 + 64], 0.0), eb, c0), r=[beb], w=[beb], partial=True)
                            if jd < 0:
                                g = j // 4
                                gat, bga = gA[g % 2]
                                gbt, bgb = gB[g % 2]
                                if j % 4 == 1:
                                    P.op("dve", lam(lambda e, gat, ea, eb: e.tensor_tensor(out=gat[:], in0=ea[:], in1=eb[:], op=ALU.add), gat, prev_e[0], eb), r=[prev_e[1], beb], w=[bga])
                                elif j % 4 == 3:
                                    P.op("dve", lam(lambda e, gbt, ea, eb: e.tensor_tensor(out=gbt[:], in0=ea[:], in1=eb[:], op=ALU.add), gbt, prev_e[0], eb), r=[prev_e[1], beb], w=[bgb])
                                    P.op("dve", lam(lambda e, gat, gbt: e.tensor_tensor(out=gat[:], in0=gat[:], in1=gbt[:], op=ALU.add), gat, gbt), r=[bga, bgb], w=[bga])

                                    def gsum(gat=gat, bga=bga, first=(g == 0)):
                                        P.op("pe", lam(lambda e, gat, first: e.matmul(pR1[:], lhsT=onesb[:], rhs=gat[:, 0, :], start=first, stop=False), gat, first), r=[b_onesb, bga], w=[b_pR1], partial=True)
                                        P.op("pe", lam(lambda e, gat, first: e.matmul(pR2[:], lhsT=onesb[:], rhs=gat[:, 1, :], start=first, stop=False), gat, first), r=[b_onesb, bga], w=[b_pR2], partial=True)

                                    pend_sums.append(gsum)
                            prev_e = (eb, beb)
                            cur = (eb, beb, j, c0)
                            if j == (0 if t == 0 else 2):
                                flush()
                        else:
                            cur = None
                        if pend is not None:
                            eb_, beb_, j_, c0_ = pend
                            first = (j_ == 0)
                            last = (j_ == nk - 1)
                            P.op("pe", lam(lambda e, eb_, j_, c0_, h, first, last: e.matmul(pO1[:, c0_:512], lhsT=vv[:, j_, h * 128:(h + 1) * 128], rhs=eb_[:, 0, c0_:512], start=first, stop=last), eb_, j_, c0_, h, first, last),
                                 r=[b_vv, beb_], w=[b_pO1], partial=True)
                            P.op("pe", lam(lambda e, eb_, j_, c0_, h, first, last: e.matmul(pO2[:, c0_:512], lhsT=vv[:, j_, h * 128:(h + 1) * 128], rhs=eb_[:, 1, c0_:512], start=first, stop=last), eb_, j_, c0_, h, first, last),
                                 r=[b_vv, beb_], w=[b_pO2], partial=True)
                            if j_ >= 4 * t:
                                rfirst = (j_ == 0)
                                P.op("pe", lam(lambda e, eb_, c0_, rfirst, last: e.matmul(pR1[:, c0_:512], lhsT=onesb[:], rhs=eb_[:, 0, c0_:512], start=rfirst, stop=last), eb_, c0_, rfirst, last),
                                     r=[b_onesb, beb_], w=[b_pR1], partial=True)
                                P.op("pe", lam(lambda e, eb_, c0_, rfirst, last: e.matmul(pR2[:, c0_:512], lhsT=onesb[:], rhs=eb_[:, 1, c0_:512], start=rfirst, stop=last), eb_, c0_, rfirst, last),
                                     r=[b_onesb, beb_], w=[b_pR2], partial=True)
                            elif pend_sums and (j_ % 4 == 0 or j_ == 4 * t - 1):
                                for f_ in pend_sums:
                                    f_()
                                pend_sums.clear()
                        pend = cur
                    obt, bob = ob[it % 2]
                    assert not pend_sums
                    P.op("dve", lambda e: e.tensor_copy(out=o1[:], in_=pO1[:]), r=[b_pO1], w=[b_o1])
                    P.op("dve", lambda e: e.tensor_copy(out=o2[:], in_=pO2[:]), r=[b_pO2], w=[b_o2])
                    P.op("dve", lambda e: e.reciprocal(out=r1[:], in_=pR1[:]), r=[b_pR1], w=[b_r1])
                    P.op("dve", lambda e: e.reciprocal(out=r2[:], in_=pR2[:]), r=[b_pR2], w=[b_r2])
                    P.op("pool", lambda e: e.tensor_tensor(out=o1[:], in0=o1[:], in1=r1[:], op=ALU.mult), r=[b_o1, b_r1], w=[b_o1])
                    P.op("pool", lambda e: e.tensor_tensor(out=o2[:], in0=o2[:], in1=r2[:], op=ALU.mult), r=[b_o2, b_r2], w=[b_o2])
                    P.op("dve", lambda e: e.scalar_tensor_tensor(out=o1[:], in0=o2[:], scalar=nlam[:, 0:1], in1=o1[:], op0=ALU.mult, op1=ALU.add), r=[b_o1, b_o2, b_nlam], w=[b_o1])
                    P.op("pool", lambda e: e.tensor_tensor(out=osq[:], in0=o1[:], in1=o1[:], op=ALU.mult), r=[b_o1], w=[b_osq])

                    def tail(obt=obt, bob=bob, h=h, t=t):
                        P.op("pe", lambda e: e.matmul(pR1[:], lhsT=onesb[:], rhs=osq[:], start=True, stop=True), r=[b_onesb, b_osq], w=[b_pR1])
                        P.op("act", lambda e: e.activation(out=ors[:], in_=pR1[:], func=AF.Ln, scale=1.0 / 128, bias=epss[:, 0:1]), r=[b_pR1, b_epss], w=[b_ors])
                        P.op("act", lambda e: e.activation(out=ors[:], in_=ors[:], func=AF.Exp, scale=-0.5), r=[b_ors], w=[b_ors])
                        P.op("dve", lam(lambda e, obt: e.scalar_tensor_tensor(out=obt[:], in0=o1[:], scalar=gsub[:, 0:1], in1=ors[:], op0=ALU.mult, op1=ALU.mult), obt), r=[b_o1, b_gsub, b_ors], w=[bob])
                        P.op("sp", lam(lambda e, obt, h, t: e.dma_start(out=oT_d[h, :, t * 512:(t + 1) * 512], in_=obt[:]), obt, h, t), r=[bob], dkey="auto")

                    deferred.append(tail)
                    it += 1
            flush()
        P.barrier()

        TC = 256
        NTC = S // TC
        with ExitStack() as pc:
            w_out, b_w_out = mk(pc, "w_out", [128, 8, D], BF16)
            w_up = [mk(pc, f"w_up{kc}", [128, 2 * FF], BF16) for kc in range(8)]
            w_dn, b_w_dn = mk(pc, "w_dn", [128, NF, D], BF16)
            gffn, b_gffn = mk(pc, "gffn", [128, D], F32)
            gfin, b_gfin = mk(pc, "gfin", [128, D], F32)
            wfc, b_wfc = mk(pc, "wfc", [128, NF * 3], F32)
            bfc, b_bfc = mk(pc, "bfc", [128, NF], F32)
            halo, b_halo = mk(pc, "halo", [128, NF, 2], F32)
            mixT, b_mixT = mk(pc, "mixT", [128, 8, TC], BF16)
            xc = [mk(pc, f"xc{i}", [128, 2, D], F32)[0] for i in range(2)]
            bxs = [[Buf(f"xc{i}_{s}") for s in range(2)] for i in range(2)]
            ssc, b_ssc = mk(pc, "ssc", [128, 2], F32)
            rsc, b_rsc = mk(pc, "rsc", [128, 2], F32)
            h2 = [mk(pc, f"h2{i}", [128, D], BF16) for i in range(2)]
            h2T, b_h2T = mk(pc, "h2T", [128, 8, TC], BF16)
            gb = [mk(pc, f"gb{i}", [128, TC + 2], F32) for i in range(2)]
            acc = [mk(pc, f"acc{i}", [128, TC], F32) for i in range(2)]
            sl = [mk(pc, f"sl{i}", [128, TC], F32) for i in range(2)]
            uT, b_uT = mk(pc, "uT", [128, NF, TC], BF16)

            pTc, b_pTc = mkp(pc, "pTc", [128, 8, 128], BF16)
            pY = [mkp(pc, f"pY{i}", [128, 2, 512], F32) for i in range(2)]
            pGV = [mkp(pc, f"pGV{i}", [128, 2, TC], F32) for i in range(2)]

            k_w2 = key("wload2")
            for kc in range(8):
                P.op("pool", lam(lambda e, kc: e.dma_start(out=w_out[:, kc, :], in_=w_out_d[kc * 128:(kc + 1) * 128, :]), kc), w=[b_w_out], partial=True, dkey="auto")
            k_wu = key("wup")
            for kc in range(8):
                wt_, bw_ = w_up[kc]
                P.op("pool", lam(lambda e, wt_, kc: e.dma_start(out=wt_[:], in_=w_up_d[kc * 128:(kc + 1) * 128, :]), wt_, kc), w=[bw_], dkey="auto")
            k_wd = key("wdn")
            for f0 in range(0, NF, 11):
                P.op("pool", lam(lambda e, f0: e.dma_start(out=w_dn[:, f0:f0 + 11, :], in_=w_down_d[f0 * 128:(f0 + 11) * 128, :].rearrange("(f p) n -> p f n", p=128)), f0),
                     w=[b_w_dn], partial=True, dkey="auto")
            P.op("sp", lambda e: e.dma_start(out=gffn[:], in_=gffn_d[0:1, :].partition_broadcast(128)), w=[b_gffn], dkey="auto")
            P.op("sp", lambda e: e.dma_start(out=gfin[:], in_=gfin_d[0:1, :].partition_broadcast(128)), w=[b_gfin], dkey="auto")
            P.op("sp", lambda e: e.dma_start(out=wfc[:], in_=wfc_d[:, :]), w=[b_wfc], dkey="auto")
            P.op("sp", lambda e: e.dma_start(out=bfc[:], in_=bfc_d[:, :]), w=[b_bfc], dkey="auto")
            P.op("pool", lambda e: e.memset(halo[:], 0.0), w=[b_halo])

            k_m = key("mix")
            k_xc = [key("xc0"), key("xc1")]
            k_o = [[key(f"o{i}{s}") for s in range(2)] for i in range(2)]

            def load_x_c(t):
                xb = xc[t % 2]
                s0 = t * TC
                P.op("sp", lam(lambda e, xb, s0: e.dma_start(out=xb[:], in_=x_d[s0:s0 + TC, :].rearrange("(s p) d -> p s d", p=128)), xb, s0), w=bxs[t % 2], dkey="auto")

            def load_mix(t):
                s0 = t * TC
                P.op("sp", lam(lambda e, s0: e.dma_start(out=mixT[:, 0:4, :], in_=aT_d[:, :, s0:s0 + TC].rearrange("c p s -> p c s")), s0), w=[b_mixT], partial=True, dkey="auto")
                P.op("sp", lam(lambda e, s0: e.dma_start(out=mixT[:, 4:8, :], in_=oT_d[:, :, s0:s0 + TC].rearrange("c p s -> p c s")), s0), w=[b_mixT], partial=True, dkey="auto")

            load_x_c(0)
            load_mix(0)
            fctr = 0
            for t in range(NTC):
                xb = xc[t % 2]
                bx = bxs[t % 2]
                if t + 1 < NTC:
                    load_x_c(t + 1)
                for s in range(2):
                    py, bpy = pY[s % 2]
                    ht, bh = h2[s % 2]
                    for half in range(2):
                        for kc in range(8):
                            P.op("pe", lam(lambda e, py, half, kc, s: e.matmul(py[:, half, :], lhsT=mixT[:, kc, s * 128:(s + 1) * 128], rhs=w_out[:, kc, half * 512:(half + 1) * 512], start=(kc == 0), stop=(kc == 7)), py, half, kc, s),
                                 r=[b_mixT, b_w_out], w=[bpy], partial=True)
                    P.op("dve", lam(lambda e, s, py, xb: e.tensor_tensor(out=xb[:, s, :], in0=py[:].rearrange("p a b -> p (a b)"), in1=xb[:, s, :], op=ALU.add), s, py, xb),
                         r=[bpy, bx[s]], w=[bx[s]])
                    P.op("act", lam(lambda e, s, xb, ht: e.activation(out=ht[:], in_=xb[:, s, :], func=AF.Square, scale=1.0 / 32, accum_out=ssc[:, s:s + 1]), s, xb, ht),
                         r=[bx[s]], w=[bh, b_ssc], partial=True)
                if t + 1 < NTC:
                    load_mix(t + 1)
                P.op("act", lambda e: e.activation(out=ssc[:], in_=ssc[:], func=AF.Sqrt, bias=epsr[:, 0:1]), r=[b_ssc, b_epsr], w=[b_ssc])
                P.op("dve", lambda e: e.reciprocal(out=rsc[:], in_=ssc[:]), r=[b_ssc], w=[b_rsc])
                for s in range(2):
                    ht, bh = h2[s % 2]
                    P.op("dve", lam(lambda e, ht, s, xb: e.scalar_tensor_tensor(out=ht[:], in0=xb[:, s, :], scalar=rsc[:, s:s + 1], in1=gffn[:], op0=ALU.mult, op1=ALU.mult), ht, s, xb),
                         r=[bx[s], b_rsc, b_gffn], w=[bh])
                    for c in range(8):
                        P.op("pe", lam(lambda e, ht, c: e.transpose(out=pTc[:, c, :], in_=ht[:, c * 128:(c + 1) * 128], identity=ident[:]), ht, c), r=[bh, b_ident], w=[b_pTc], partial=True)
                    P.op("act", lam(lambda e, s: e.copy(out=h2T[:, :, s * 128:(s + 1) * 128], in_=pTc[:]), s), r=[b_pTc], w=[b_h2T], partial=True)
                for f in range(NF):
                    i = fctr % 2
                    fctr += 1
                    pgv, bpgv = pGV[i]
                    gbt, bgb = gb[i]
                    at, bat = acc[i]
                    st, bst = sl[i]
                    for kc in range(8):
                        wt_, bw_ = w_up[kc]
                        P.op("pe", lam(lambda e, pgv, wt_, kc, f: e.matmul(pgv[:, 0, :], lhsT=wt_[:, f * 128:(f + 1) * 128], rhs=h2T[:, kc, :], start=(kc == 0), stop=(kc == 7)), pgv, wt_, kc, f),
                             r=[bw_, b_h2T], w=[bpgv], partial=True)
                    for kc in range(8):
                        wt_, bw_ = w_up[kc]
                        P.op("pe", lam(lambda e, pgv, wt_, kc, f: e.matmul(pgv[:, 1, :], lhsT=wt_[:, FF + f * 128:FF + (f + 1) * 128], rhs=h2T[:, kc, :], start=(kc == 0), stop=(kc == 7)), pgv, wt_, kc, f),
                             r=[bw_, b_h2T], w=[bpgv], partial=True)
                    P.op("pool", lam(lambda e, gbt, f: e.tensor_copy(out=gbt[:, 0:2], in_=halo[:, f, :]), gbt, f), r=[b_halo], w=[bgb], partial=True)
                    P.op("act", lam(lambda e, gbt, pgv: e.copy(out=gbt[:, 2:TC + 2], in_=pgv[:, 0, :]), gbt, pgv), r=[bpgv], w=[bgb], partial=True)
                    P.op("pool", lam(lambda e, gbt, f: e.tensor_copy(out=halo[:, f, :], in_=gbt[:, TC:TC + 2]), gbt, f), r=[bgb], w=[b_halo], partial=True)
                    P.op("dve", lam(lambda e, at, gbt, f: e.tensor_scalar(out=at[:], in0=gbt[:, 0:TC], scalar1=wfc[:, 3 * f:3 * f + 1], scalar2=bfc[:, f:f + 1], op0=ALU.mult, op1=ALU.add), at, gbt, f),
                         r=[bgb, b_wfc, b_bfc], w=[bat])
                    P.op("dve", lam(lambda e, at, gbt, f: e.scalar_tensor_tensor(out=at[:], in0=gbt[:, 1:TC + 1], scalar=wfc[:, 3 * f + 1:3 * f + 2], in1=at[:], op0=ALU.mult, op1=ALU.add), at, gbt, f),
                         r=[bgb, bat], w=[bat])
                    P.op("dve", lam(lambda e, at, gbt, f: e.scalar_tensor_tensor(out=at[:], in0=gbt[:, 2:TC + 2], scalar=wfc[:, 3 * f + 2:3 * f + 3], in1=at[:], op0=ALU.mult, op1=ALU.add), at, gbt, f),
                         r=[bgb, bat], w=[bat])
                    P.op("act", lam(lambda e, st, at: e.activation(out=st[:], in_=at[:], func=AF.Silu), st, at), r=[bat], w=[bst])
                    P.op("dve", lam(lambda e, f, st, pgv: e.tensor_tensor(out=uT[:, f, :], in0=st[:], in1=pgv[:, 1, :], op=ALU.mult), f, st, pgv), r=[bst, bpgv], w=[b_uT], partial=True)
                for s in range(2):
                    py, bpy = pY[s % 2]
                    ht, bh = h2[s % 2]
                    for half in range(2):
                        for f in range(NF):
                            P.op("pe", lam(lambda e, py, half, f, s: e.matmul(py[:, half, :], lhsT=uT[:, f, s * 128:(s + 1) * 128], rhs=w_dn[:, f, half * 512:(half + 1) * 512], start=(f == 0), stop=(f == NF - 1)), py, half, f, s),
                                 r=[b_uT, b_w_dn], w=[bpy], partial=True)
                    P.op("dve", lam(lambda e, s, py, xb: e.tensor_tensor(out=xb[:, s, :], in0=py[:].rearrange("p a b -> p (a b)"), in1=xb[:, s, :], op=ALU.add), s, py, xb),
                         r=[bpy, bx[s]], w=[bx[s]])
                    P.op("act", lam(lambda e, s, xb, ht: e.activation(out=ht[:], in_=xb[:, s, :], func=AF.Square, scale=1.0 / 32, accum_out=ssc[:, s:s + 1]), s, xb, ht),
                         r=[bx[s]], w=[bh, b_ssc], partial=True)
                P.op("act", lambda e: e.activation(out=ssc[:], in_=ssc[:], func=AF.Sqrt, bias=epsr[:, 0:1]), r=[b_ssc, b_epsr], w=[b_ssc])
                P.op("dve", lambda e: e.reciprocal(out=rsc[:], in_=ssc[:]), r=[b_ssc], w=[b_rsc])
                for s in range(2):
                    P.op("dve", lam(lambda e, xb, s: e.scalar_tensor_tensor(out=xb[:, s, :], in0=xb[:, s, :], scalar=rsc[:, s:s + 1], in1=gfin[:], op0=ALU.mult, op1=ALU.mult), xb, s),
                         r=[bx[s], b_rsc, b_gfin], w=[bx[s]])
                    P.op("sp", lam(lambda e, xb, t, s: e.dma_start(out=out_d[t * TC + s * 128:t * TC + (s + 1) * 128, :], in_=xb[:, s, :]), xb, t, s), r=[bx[s]], dkey="auto")
        P.barrier()
        P.op("sp", lambda e: e.nop())

        engsem = {e: top.enter_context(nc.semaphore("s_" + e)) for e in ("act", "dve", "pool", "pe")}
        dmasem = {k: top.enter_context(nc.semaphore("d_" + k)) for k in P.keys}
        run = P.emitter(engsem, dmasem)
        with nc.Block() as block:
            @block.sync
            def _(e):
                run("sp", e)

            @block.scalar
            def _(e):
                run("act", e)

            @block.vector
            def _(e):
                run("dve", e)

            @block.gpsimd
            def _(e):
                run("pool", e)

            @block.tensor
            def _(e):
                run("pe", e)
    return nc


def host_shared(inp, S):
    f32 = np.float32
    w_in = np.asarray(inp["w_in"], f32)[0]
    cv, cg, q, k, v = np.split(w_in, [512, 1024, 1536, 2048], axis=1)

    def swap(w):
        w4 = w.reshape(D, 4, 2, 64)
        return np.concatenate([w4[..., 32:], w4[..., :32]], axis=-1).reshape(D, 512)

    w_in_p = np.ascontiguousarray(np.concatenate([cv, cg, q, swap(q), k, swap(k), v], axis=1))
    inv_freq = (1.0 / (np.float32(10000.0) ** (np.arange(0, 64, 2, dtype=f32) / np.float32(64)))).astype(f32)
    ang = (np.arange(S, dtype=f32)[:, None] * inv_freq[None, :]).astype(f32)
    cos = np.cos(ang.astype(np.float64)).astype(f32).T
    sin = np.sin(ang.astype(np.float64)).astype(f32).T
    cosT = np.ascontiguousarray(np.concatenate([cos, cos, cos, cos], axis=0))
    sinT = np.ascontiguousarray(np.concatenate([-sin, sin, -sin, sin], axis=0))

    def col4(vv):
        return np.asarray(vv, f32).reshape(4, 128).T

    cvec = np.ascontiguousarray(np.concatenate([col4(inp["b_dw"][0]), col4(inp["ln_g"][0]), col4(inp["ln_b"][0])], axis=1))
    lamv = np.ascontiguousarray(np.concatenate([np.asarray(inp[n], f32)[0] for n in ("lambda_q1", "lambda_k1", "lambda_q2", "lambda_k2")])[None, :])
    wfc = np.ascontiguousarray(np.asarray(inp["w_fconv"], f32)[0].T.reshape(NF, 128, 3).transpose(1, 0, 2).reshape(128, NF * 3))
    bfc = np.ascontiguousarray(np.asarray(inp["b_fconv"], f32)[0].reshape(NF, 128).T)
    return {
        "w_in_p": w_in_p,
        "w_out": np.ascontiguousarray(np.asarray(inp["w_out"], f32)[0]),
        "w_up": np.ascontiguousarray(np.asarray(inp["w_up"], f32)[0]),
        "w_down": np.ascontiguousarray(np.asarray(inp["w_down"], f32)[0]),
        "cosT": cosT,
        "sinT": sinT,
        "ident": np.eye(128, dtype=f32),
        "g_mix": np.ascontiguousarray(np.asarray(inp["g_mix"], f32)[0][None, :]),
        "g_ffn": np.ascontiguousarray(np.asarray(inp["g_ffn"], f32)[0][None, :]),
        "g_final": np.ascontiguousarray(np.asarray(inp["g_final"], f32)[None, :]),
        "wdw_t": np.ascontiguousarray(np.asarray(inp["w_dw"], f32)[0].T),
        "cvec": cvec,
        "lamv": lamv,
        "g_subln": np.ascontiguousarray(np.asarray(inp["g_subln"], f32)[0][:, None]),
        "wfc": wfc,
        "bfc": bfc,
    }


def kernel(**inputs):
    x = np.asarray(inputs["x"], np.float32)
    B, S, _ = x.shape
    shared = host_shared(inputs, S)
    nc = build_nc(S)
    in_maps = []
    for b in range(B):
        m = dict(shared)
        m["x"] = np.ascontiguousarray(x[b])
        in_maps.append(m)
    res = run_bass_kernel_spmd(nc, in_maps, core_ids=list(range(B)))
    return np.stack([np.asarray(r["out"], np.float32) for r in res.results], axis=0)
```
